# Optimizing a Trainium2 kernel written in Bass

```python
import math
import jax, jax.numpy as jnp
from jax import lax
import numpy as np


D_MODEL = 2048
BATCH = 2
SEQ = 4096
DEPTH = 4
DEC_BATCH = 8
DEC_SEQ = 4096
PAST_LEN = 128

N_MIXERS = 3
N_LAYERS_A = len(range(0, DEPTH, N_MIXERS))
N_LAYERS_B = len(range(1, DEPTH, N_MIXERS))
N_LAYERS_C = len(range(2, DEPTH, N_MIXERS))
S5_GROUP_CH = 16
S5_GROUPS = D_MODEL // S5_GROUP_CH
S5_STATE = 64
S5_DT_MIN = 1e-3
S5_DT_MAX = 1e-1
POOL_WINDOWS = (2, 4, 8, 16)
POOL_GROUPS = len(POOL_WINDOWS)
POOL_GROUP_CH = D_MODEL // POOL_GROUPS
HEAD_DIM = 128
N_HEADS = D_MODEL // HEAD_DIM
N_KV_HEADS = 4
GQA_GROUP = N_HEADS // N_KV_HEADS
QKV_DIM = (N_HEADS + 2 * N_KV_HEADS) * HEAD_DIM
Q_BLOCK = 128
GRID_W = 64
ROPE_THETA = 10000.0
ROPE_FREQS = HEAD_DIM // 4
D_FF = 4 * D_MODEL
DEEPNORM_ALPHA = (2 * DEPTH) ** 0.25
DEEPNORM_BETA = (8 * DEPTH) ** -0.25
LN_EPS = 1e-5
RMS_EPS = 1e-6

kernel_name = 'hybrid_s5_pool_axialgqa_deepnorm_encoder'


def _layer_norm(x, g, b):
    xf = x.astype(jnp.float32)
    mu = jnp.mean(xf, axis=-1, keepdims=True)
    xc = xf - mu
    var = jnp.mean(xc * xc, axis=-1, keepdims=True)
    y = xc * lax.rsqrt(var + LN_EPS) * g.astype(jnp.float32) + b.astype(jnp.float32)
    return y.astype(x.dtype)


def _rms_norm(x, g):
    xf = x.astype(jnp.float32)
    y = xf * lax.rsqrt(jnp.mean(xf * xf, axis=-1, keepdims=True) + RMS_EPS) * g.astype(jnp.float32)
    return y.astype(x.dtype)


def _axial_rope_tables(seq_len):
    rows = seq_len // GRID_W
    row = jnp.repeat(jnp.arange(rows, dtype=jnp.float32), GRID_W)
    col = jnp.broadcast_to(jnp.arange(GRID_W, dtype=jnp.float32), (rows, GRID_W)).reshape(-1)
    inv_freq = jnp.power(ROPE_THETA, -jnp.arange(ROPE_FREQS, dtype=jnp.float32) / ROPE_FREQS)
    ang = jnp.stack([row[:, None] * inv_freq, col[:, None] * inv_freq], axis=1)
    return jnp.cos(ang)[:, None], jnp.sin(ang)[:, None]


def _apply_axial_rope(x, cos, sin):
    b, l, h, _ = x.shape
    xr = x.astype(jnp.float32).reshape(b, l, h, 2, 2, ROPE_FREQS)
    x1 = xr[..., 0, :]
    x2 = xr[..., 1, :]
    out = jnp.stack([x1 * cos - x2 * sin, x2 * cos + x1 * sin], axis=-2)
    return out.reshape(x.shape).astype(x.dtype)


def _ssm_combine(left, right):
    a1, b1 = left
    a2, b2 = right
    return a1 * a2, a2 * b1 + b2


def _s5_mixer(x, a_re, a_im, log_step, b_re, b_im, c_re, c_im, d_skip, w_out, w_gate):
    bsz, seq_len, _ = x.shape
    f32 = jnp.float32
    u = x.astype(f32)
    ug = u.reshape(bsz, seq_len, S5_GROUPS, S5_GROUP_CH)
    y = d_skip.astype(f32) * u
    for direction in range(2):
        lam = lax.complex(jnp.minimum(a_re[direction].astype(f32), -1e-4), a_im[direction].astype(f32))
        dt = jnp.exp(log_step[direction].astype(f32))[:, None]
        lam_bar = jnp.exp(lam * dt)
        b = lax.complex(b_re[direction].astype(f32), b_im[direction].astype(f32))
        b_bar = ((lam_bar - 1.0) / lam)[..., None] * b
        bu = lax.complex(jnp.einsum('blgp,gnp->blgn', ug, jnp.real(b_bar)),
                         jnp.einsum('blgp,gnp->blgn', ug, jnp.imag(b_bar)))
        a_seq = jnp.broadcast_to(lam_bar, (1, seq_len) + lam_bar.shape)
        _, h = lax.associative_scan(_ssm_combine, (a_seq, bu), reverse=(direction == 1), axis=1)
        yc = (jnp.einsum('blgn,gpn->blgp', jnp.real(h), c_re[direction].astype(f32))
              - jnp.einsum('blgn,gpn->blgp', jnp.imag(h), c_im[direction].astype(f32)))
        y = y + yc.reshape(bsz, seq_len, D_MODEL)
    g = jax.nn.gelu(y)
    out = (g @ w_out.astype(f32)) * jax.nn.sigmoid(g @ w_gate.astype(f32))
    return out.astype(x.dtype)


def _pool_mixer(x, pool_w, pool_scale):
    bsz, seq_len, _ = x.shape
    xf = x.astype(jnp.float32)
    cs = jnp.concatenate([jnp.zeros((bsz, 1, D_MODEL), jnp.float32), jnp.cumsum(xf, axis=1)], axis=1)
    t = jnp.arange(seq_len)
    pooled = []
    for gi, w in enumerate(POOL_WINDOWS):
        lo = jnp.clip(t - w // 2, 0, seq_len)
        hi = jnp.clip(t + w // 2, 0, seq_len)
        csg = cs[..., gi * POOL_GROUP_CH:(gi + 1) * POOL_GROUP_CH]
        cnt = (hi - lo).astype(jnp.float32)[None, :, None]
        pooled.append((csg[:, hi] - csg[:, lo]) / cnt)
    p = jnp.stack(pooled, axis=2) - xf.reshape(bsz, seq_len, POOL_GROUPS, POOL_GROUP_CH)
    out = jnp.einsum('blgc,gcd->blgd', p, pool_w.astype(jnp.float32)).reshape(bsz, seq_len, D_MODEL)
    out = out * pool_scale.astype(jnp.float32)
    return out.astype(x.dtype)


def _attention_mixer(x, cos, sin, w_qkv, q_norm, k_norm, w_o):
    bsz, seq_len, _ = x.shape
    qkv = x @ w_qkv
    q, k, v = jnp.split(qkv, [N_HEADS * HEAD_DIM, (N_HEADS + N_KV_HEADS) * HEAD_DIM], axis=-1)
    q = q.reshape(bsz, seq_len, N_HEADS, HEAD_DIM)
    k = k.reshape(bsz, seq_len, N_KV_HEADS, HEAD_DIM)
    v = v.reshape(bsz, seq_len, N_KV_HEADS, HEAD_DIM)
    q = _apply_axial_rope(_rms_norm(q, q_norm), cos, sin)
    k = _apply_axial_rope(_rms_norm(k, k_norm), cos, sin)
    n_blk = seq_len // Q_BLOCK
    qb = q.reshape(bsz, n_blk, Q_BLOCK, N_KV_HEADS, GQA_GROUP, HEAD_DIM).transpose(1, 0, 3, 4, 2, 5)
    scale = HEAD_DIM ** -0.5

    def attend(q_blk):
        s = jnp.einsum('bkgqd,bskd->bkgqs', q_blk, k).astype(jnp.float32) * scale
        p = jax.nn.softmax(s, axis=-1).astype(v.dtype)
        return jnp.einsum('bkgqs,bskd->bkgqd', p, v)

    o = lax.map(attend, qb)
    o = o.transpose(1, 0, 4, 2, 3, 5).reshape(bsz, seq_len, D_MODEL)
    return o @ w_o


def _sq_relu_mlp(x, w1, w2):
    h = jax.nn.relu(x @ w1)
    return (h * h) @ w2


def _encoder(x, s5_a_re, s5_a_im, s5_log_step, s5_b_re, s5_b_im, s5_c_re, s5_c_im, s5_d,
             s5_w_out, s5_w_gate, pool_w, pool_scale, attn_w_qkv, attn_q_norm, attn_k_norm,
             attn_w_o, ln1_g, ln1_b, ln2_g, ln2_b, mlp_w1, mlp_w2):
    cos, sin = _axial_rope_tables(x.shape[1])
    for i in range(DEPTH):
        kind = i % N_MIXERS
        j = i // N_MIXERS
        if kind == 0:
            h = _s5_mixer(x, s5_a_re[j], s5_a_im[j], s5_log_step[j], s5_b_re[j], s5_b_im[j],
                          s5_c_re[j], s5_c_im[j], s5_d[j], s5_w_out[j], s5_w_gate[j])
        elif kind == 1:
            h = _pool_mixer(x, pool_w[j], pool_scale[j])
        else:
            h = _attention_mixer(x, cos, sin, attn_w_qkv[j], attn_q_norm[j], attn_k_norm[j], attn_w_o[j])
        x = _layer_norm(DEEPNORM_ALPHA * x + h, ln1_g[i], ln1_b[i])
        x = _layer_norm(DEEPNORM_ALPHA * x + _sq_relu_mlp(x, mlp_w1[i], mlp_w2[i]), ln2_g[i], ln2_b[i])
    return x


def setup_inputs(seed: int = 0) -> dict:
    key = jax.random.key(seed)
    ks = jax.random.split(key, 32)
    f32 = jnp.float32

    def nrm(k, shape, std):
        return std * jax.random.normal(k, shape, f32)

    n_idx = jnp.arange(S5_STATE, dtype=f32)
    a_shape = (N_LAYERS_A, 2, S5_GROUPS, S5_STATE)
    return {
        'x_prompt': nrm(ks[0], (BATCH, SEQ, D_MODEL), 1.0),
        'x_sample': nrm(ks[1], (DEC_BATCH, DEC_SEQ, D_MODEL), 1.0),
        's5_a_re': -0.5 + nrm(ks[2], a_shape, 0.01),
        's5_a_im': math.pi * n_idx + nrm(ks[3], a_shape, 0.01),
        's5_log_step': jax.random.uniform(ks[4], (N_LAYERS_A, 2, S5_GROUPS), f32,
                                          math.log(S5_DT_MIN), math.log(S5_DT_MAX)),
        's5_b_re': nrm(ks[5], (N_LAYERS_A, 2, S5_GROUPS, S5_STATE, S5_GROUP_CH), (2 * S5_GROUP_CH) ** -0.5),
        's5_b_im': nrm(ks[6], (N_LAYERS_A, 2, S5_GROUPS, S5_STATE, S5_GROUP_CH), (2 * S5_GROUP_CH) ** -0.5),
        's5_c_re': nrm(ks[7], (N_LAYERS_A, 2, S5_GROUPS, S5_GROUP_CH, S5_STATE), 0.5 ** 0.5),
        's5_c_im': nrm(ks[8], (N_LAYERS_A, 2, S5_GROUPS, S5_GROUP_CH, S5_STATE), 0.5 ** 0.5),
        's5_d': nrm(ks[9], (N_LAYERS_A, D_MODEL), 1.0),
        's5_w_out': nrm(ks[10], (N_LAYERS_A, D_MODEL, D_MODEL), D_MODEL ** -0.5 * DEEPNORM_BETA),
        's5_w_gate': nrm(ks[11], (N_LAYERS_A, D_MODEL, D_MODEL), D_MODEL ** -0.5),
        'pool_w': nrm(ks[12], (N_LAYERS_B, POOL_GROUPS, POOL_GROUP_CH, POOL_GROUP_CH),
                      POOL_GROUP_CH ** -0.5 * DEEPNORM_BETA),
        'pool_scale': 1.0 + nrm(ks[13], (N_LAYERS_B, D_MODEL), 0.02),
        'attn_w_qkv': nrm(ks[14], (N_LAYERS_C, D_MODEL, QKV_DIM), D_MODEL ** -0.5),
        'attn_q_norm': 1.0 + nrm(ks[15], (N_LAYERS_C, HEAD_DIM), 0.02),
        'attn_k_norm': 1.0 + nrm(ks[16], (N_LAYERS_C, HEAD_DIM), 0.02),
        'attn_w_o': nrm(ks[17], (N_LAYERS_C, D_MODEL, D_MODEL), D_MODEL ** -0.5 * DEEPNORM_BETA),
        'ln1_g': 1.0 + nrm(ks[18], (DEPTH, D_MODEL), 0.02),
        'ln1_b': nrm(ks[19], (DEPTH, D_MODEL), 0.02),
        'ln2_g': 1.0 + nrm(ks[20], (DEPTH, D_MODEL), 0.02),
        'ln2_b': nrm(ks[21], (DEPTH, D_MODEL), 0.02),
        'mlp_w1': nrm(ks[22], (DEPTH, D_MODEL, D_FF), D_MODEL ** -0.5),
        'mlp_w2': nrm(ks[23], (DEPTH, D_FF, D_MODEL), D_FF ** -0.5 * DEEPNORM_BETA),
    }


def reference(x_prompt, x_sample, s5_a_re, s5_a_im, s5_log_step, s5_b_re, s5_b_im, s5_c_re,
              s5_c_im, s5_d, s5_w_out, s5_w_gate, pool_w, pool_scale, attn_w_qkv, attn_q_norm,
              attn_k_norm, attn_w_o, ln1_g, ln1_b, ln2_g, ln2_b, mlp_w1, mlp_w2):
    y_prompt = _encoder(x_prompt, s5_a_re, s5_a_im, s5_log_step, s5_b_re, s5_b_im, s5_c_re,
                        s5_c_im, s5_d, s5_w_out, s5_w_gate, pool_w, pool_scale, attn_w_qkv,
                        attn_q_norm, attn_k_norm, attn_w_o, ln1_g, ln1_b, ln2_g, ln2_b,
                        mlp_w1, mlp_w2)
    y_sample = _encoder(x_sample, s5_a_re, s5_a_im, s5_log_step, s5_b_re, s5_b_im, s5_c_re,
                        s5_c_im, s5_d, s5_w_out, s5_w_gate, pool_w, pool_scale, attn_w_qkv,
                        attn_q_norm, attn_k_norm, attn_w_o, ln1_g, ln1_b, ln2_g, ln2_b,
                        mlp_w1, mlp_w2)
    return (y_prompt, y_sample)
```

```python
import math
from contextlib import ExitStack
import numpy as np
import ml_dtypes
import concourse.bass as bass
import concourse.mybir as mybir
from concourse.bass_utils import run_bass_kernel_spmd

F32 = mybir.dt.float32
BF16 = mybir.dt.bfloat16
AF = mybir.ActivationFunctionType
ALU = mybir.AluOpType

D = 2048
L = 4096
DFF = 8192
NKC = 16
BLK = 512
NBLK = L // BLK
ALPHA = (2 * 4) ** 0.25
LN_EPS = 1e-5
RMS_EPS = 1e-6
PAD = 8
XW = BLK + 2 * PAD
NSTEP = 9
GC = 1.5957691216057308


class Prog:
    NDMA = 24

    def __init__(self, nc, es):
        self.nc = nc
        self.engs = dict(pe=nc.tensor, act=nc.scalar, dve=nc.vector, pool=nc.gpsimd, sp=nc.sync)
        self.sems = {}
        for e in ['pe', 'act', 'dve', 'pool']:
            self.sems[('c', e)] = es.enter_context(nc.semaphore("c_" + e))
        for i in range(self.NDMA):
            self.sems[('d', i)] = es.enter_context(nc.semaphore("d%d" % i))
        self.tot = {k: 0 for k in self.sems}
        self.q = {e: [] for e in self.engs}
        self.waited = {e: {} for e in self.engs}
        self.res = {}
        self.rr = 0
        self.rr2 = 0
        self.nops = 0

    def _wait(self, eng, key, val):
        if val <= 0:
            return
        if self.waited[eng].get(key, 0) < val:
            self.q[eng].append(('w', key, val))
            self.waited[eng][key] = val

    def _deps(self, eng, reads, writes):
        for r in reads:
            st = self.res.get(r)
            if st and st[0]:
                self._wait(eng, *st[0])
        for w in writes:
            st = self.res.get(w)
            if st:
                if st[0]:
                    self._wait(eng, *st[0])
                for k, v in st[1].items():
                    self._wait(eng, k, v)

    def _update(self, tok, reads, writes):
        for r in reads:
            st = self.res.setdefault(r, [None, {}])
            if st[1].get(tok[0], 0) < tok[1]:
                st[1][tok[0]] = tok[1]
        for w in writes:
            self.res[w] = [tok, {}]

    def op(self, eng, fns, reads=(), writes=()):
        if not isinstance(fns, (list, tuple)):
            fns = [fns]
        self._deps(eng, reads, writes)
        key = ('c', eng)
        self.tot[key] += 1
        tok = (key, self.tot[key])
        self.q[eng].append(('i', list(fns), key, 1))
        self._update(tok, reads, writes)
        self.nops += len(fns)

    def dma(self, queue, fn, reads=(), writes=(), grp=0):
        if grp == 0:
            i = self.rr
            self.rr = (self.rr + 1) % (self.NDMA - 8)
        else:
            i = self.NDMA - 8 + self.rr2
            self.rr2 = (self.rr2 + 1) % 8
        key = ('d', i)
        if grp == 0:
            self._wait(queue, key, self.tot[key])
        self._deps(queue, reads, writes)
        self.tot[key] += 16
        tok = (key, self.tot[key])
        self.q[queue].append(('i', [fn], key, 16))
        self._update(tok, reads, writes)
        self.nops += 1

    def barrier(self):
        for e in self.engs:
            for k, v in self.tot.items():
                self._wait(e, k, v)
        self.res = {}

    def flush(self):
        nc = self.nc
        with nc.Block() as blk:
            def runner(items, sems):
                def body(eng):
                    for it in items:
                        if it[0] == 'w':
                            eng.wait_ge(sems[it[1]], it[2])
                        else:
                            last = None
                            for f in it[1]:
                                last = f(eng)
                            last.then_inc(sems[it[2]], it[3])
                return body
            deco = dict(pe=blk.tensor, act=blk.scalar, dve=blk.vector, pool=blk.gpsimd, sp=blk.sync)
            for e in self.engs:
                if self.q[e]:
                    deco[e](runner(self.q[e], self.sems))
        self.q = {e: [] for e in self.engs}


def TT(out, a, b, op):
    return lambda e: e.tensor_tensor(out=out, in0=a, in1=b, op=op)


def TS(out, a, s1, op0, s2=None, op1=None):
    if op1 is None:
        return lambda e: e.tensor_scalar(out=out, in0=a, scalar1=s1, scalar2=None, op0=op0)
    return lambda e: e.tensor_scalar(out=out, in0=a, scalar1=s1, scalar2=s2, op0=op0, op1=op1)


def STT(out, a, s, b, op0, op1):
    return lambda e: e.scalar_tensor_tensor(out=out, in0=a, scalar=s, in1=b, op0=op0, op1=op1)


def CP(out, a):
    return lambda e: e.tensor_copy(out=out, in_=a)


def ACTF(out, a, func, scale=1.0, bias=0.0):
    return lambda e: e.activation(out=out, in_=a, func=func, bias=bias, scale=scale)


def MM(out, lhsT, rhs, start=True, stop=True):
    return lambda e: e.matmul(out, lhsT, rhs, start=start, stop=stop)


def TR(out, a, ident):
    return lambda e: e.transpose(out, a, ident)


def DMA(out, a, slow=False):
    if slow:
        return lambda e: e.dma_start(out=out, in_=a, allow_slow_non_contiguous=True)
    return lambda e: e.dma_start(out=out, in_=a)


def MEMSET(out, v):
    return lambda e: e.memset(out, v)


class Builder:
    def __init__(self, nseq=2, layers=(0, 1, 2, 3), nblk=NBLK, do_prep=True, debug_y=False):
        self.nseq = nseq
        self.layers = tuple(layers)
        self.nblk = nblk
        self.do_prep = do_prep
        self.debug_y = debug_y

    def declare(self):
        nc = self.nc
        ns = self.nseq

        def inp(name, shape, dt=F32):
            return nc.dram_tensor(name, list(shape), dt, kind="ExternalInput").ap()

        def scr(name, shape, dt=F32):
            return nc.dram_tensor(name, list(shape), dt, kind="Internal").ap()
        d = {}
        d['x_in'] = inp('x_in', [ns, L, D])
        d['y_out'] = nc.dram_tensor('y_out', [ns, L, D], F32, kind="ExternalOutput").ap()
        for nm, shp in [('s5_a_re', [2, 2, 128, 64]), ('s5_a_im', [2, 2, 128, 64]), ('s5_log_step', [2, 2, 128]),
                        ('s5_b_re', [2, 2, 128, 64, 16]), ('s5_b_im', [2, 2, 128, 64, 16]),
                        ('s5_c_re', [2, 2, 128, 16, 64]), ('s5_c_im', [2, 2, 128, 16, 64]),
                        ('s5_d', [2, D]), ('s5_w_out', [2, D, D]), ('s5_w_gate', [2, D, D]),
                        ('pool_w', [1, 4, 512, 512]), ('pool_scale', [1, D]),
                        ('attn_w_qkv', [1, D, 3072]), ('attn_q_norm', [1, 128]), ('attn_k_norm', [1, 128]),
                        ('attn_w_o', [1, D, D]), ('ln1_g', [4, D]), ('ln1_b', [4, D]), ('ln2_g', [4, D]),
                        ('ln2_b', [4, D]), ('mlp_w1', [4, D, DFF]), ('mlp_w2', [4, DFF, D])]:
            d[nm] = inp(nm, shp)
        d['c_ident'] = inp('c_ident', [128, 128])
        d['c_identb'] = inp('c_identb', [128, 128], BF16)
        d['c_swap'] = inp('c_swap', [128, 128])
        d['c_rot'] = inp('c_rot', [128, 128])
        d['c_ones_ln'] = inp('c_ones_ln', [128, 128])
        d['c_ones_rms'] = inp('c_ones_rms', [128, 128])
        d['c_onesb'] = inp('c_onesb', [128, 128], BF16)
        d['c_maskf'] = inp('c_maskf', [128, 128])
        d['c_maskb'] = inp('c_maskb', [128, 128])
        d['c_sel'] = inp('c_sel', [16, 128])
        d['c_cos'] = inp('c_cos', [128, L])
        d['c_sin'] = inp('c_sin', [128, L])
        d['c_invc'] = inp('c_invc', [128, 4, L])
        d['actA'] = scr('actA', [ns, L, D])
        d['actB'] = scr('actB', [ns, L, D])
        d['ys5'] = scr('ys5', [ns, L, D])
        d['w1b'] = scr('w1b', [4, D, DFF], BF16)
        d['w2b'] = scr('w2b', [4, DFF, D], BF16)
        d['woutb'] = scr('woutb', [2, D, D], BF16)
        d['wgateb'] = scr('wgateb', [2, D, D], BF16)
        d['poolwb'] = scr('poolwb', [4, 512, 512], BF16)
        d['wqkvb'] = scr('wqkvb', [D, 3072], BF16)
        d['wob'] = scr('wob', [D, D], BF16)
        d['kts'] = scr('kts', [ns, 4, 128, L], BF16)
        d['vs'] = scr('vs', [ns, 4, 128, 32, 128], BF16)
        d['rt32'] = scr('rt32', [2, 2, 128, 128 * 128])
        d['ry32'] = scr('ry32', [2, 2, 128, 128 * 128])
        d['ot32'] = scr('ot32', [2, 2, 128, 128 * 128])
        d['rt16'] = scr('rt16', [2, 2, 128, 128 * 128], BF16)
        d['ot16'] = scr('ot16', [2, 2, 128, 128 * 128], BF16)
        d['mt16'] = scr('mt16', [2, 128, 128, 128], BF16)
        d['pm16'] = scr('pm16', [2, 2, 128, 128, NSTEP * 128], BF16)
        self.d = d

    def sb(self, es, name, shape, dt=F32):
        self._uid = getattr(self, '_uid', 0) + 1
        return es.enter_context(self.nc.sbuf_tensor("%s_u%d" % (name, self._uid), list(shape), dt))

    def next_bank(self):
        b = self.bank_rr
        self.bank_rr = (self.bank_rr + 1) % len(self.banks)
        return b

    def build(self):
        nc = bass.Bass("TRN2", target_bir_lowering=False)
        self.nc = nc
        self.declare()
        with ExitStack() as es:
            P = Prog(nc, es)
            self.P = P
            self.banks = [es.enter_context(nc.psum_tensor("ps%d" % i, [128, 512], F32)) for i in range(7)]
            self.bank16 = es.enter_context(nc.psum_tensor("ps16", [128, 1024], BF16))
            self.bank_rr = 0
            c = {}
            c['ident'] = self.sb(es, 'k_ident', [128, 128])
            c['identb'] = self.sb(es, 'k_identb', [128, 128], BF16)
            c['ones_ln'] = self.sb(es, 'k_ones_ln', [128, 128])
            c['ones_rms'] = self.sb(es, 'k_ones_rms', [128, 128])
            c['onesb'] = self.sb(es, 'k_onesb', [128, 128], BF16)
            c['rot'] = self.sb(es, 'k_rot', [128, 128])
            c['lnp'] = self.sb(es, 'k_lnp', [128, 4, 4, NKC])
            c['pscale'] = self.sb(es, 'k_pscale', [128, NKC])
            c['qkn'] = self.sb(es, 'k_qkn', [128, 2])
            c['eps'] = self.sb(es, 'k_eps', [128, 2])
            self.c = c
            d = self.d
            P.op('dve', [MEMSET(c['eps'][:, 0:1], LN_EPS), MEMSET(c['eps'][:, 1:2], RMS_EPS)], writes=['k_eps'])
            for nm in ['ident', 'identb', 'ones_ln', 'ones_rms', 'onesb', 'rot']:
                P.dma('sp', DMA(c[nm][:], d['c_' + nm]), writes=['k_' + nm])
            for wi, nm in enumerate(['ln1_g', 'ln1_b', 'ln2_g', 'ln2_b']):
                for l in range(4):
                    P.dma('sp', DMA(c['lnp'][:, wi, l, :], d[nm][l].rearrange("(k p) -> p k", p=128), slow=True),
                          writes=['k_lnp'])
            P.dma('sp', DMA(c['pscale'][:], d['pool_scale'][0].rearrange("(k p) -> p k", p=128), slow=True),
                  writes=['k_pscale'])
            P.dma('sp', DMA(c['qkn'][:, 0:1], d['attn_q_norm'][0].rearrange("(p o) -> p o", o=1), slow=True),
                  writes=['k_qkn'])
            P.dma('sp', DMA(c['qkn'][:, 1:2], d['attn_k_norm'][0].rearrange("(p o) -> p o", o=1), slow=True),
                  writes=['k_qkn'])
            if self.do_prep:
                self.convert_weights()
                if 0 in self.layers or 3 in self.layers:
                    self.s5_prep()
                else:
                    P.barrier()
                    P.flush()
            cur = d['x_in']
            pp = [d['actA'], d['actB']]
            for li, l in enumerate(self.layers):
                src = cur
                dst = d['y_out'] if li == len(self.layers) - 1 else pp[li % 2]
                cur = dst
                if l % 3 == 0:
                    self.s5_scan(l // 3, src)
                if l % 3 == 2:
                    self.kv_phase(src)
                if not (self.debug_y and l % 3 == 0):
                    self.block_phase(l, src, dst)
                else:
                    self.copy_debug(d['ys5'], dst)
            P.barrier()
            P.flush()
        return nc

    def convert_weights(self):
        P, d = self.P, self.d

        def conv(dst, src, rows, rstep):
            for r0 in range(0, rows, rstep):
                P.dma('pool', DMA(dst[r0:r0 + rstep], src[r0:r0 + rstep]), writes=['wscr'], grp=1)
        need_mlp = sorted(set(self.layers))
        for l in need_mlp:
            conv(d['w1b'][l], d['mlp_w1'][l], D, 256)
            conv(d['w2b'][l], d['mlp_w2'][l], DFF, 1024)
            if l % 3 == 0:
                j = l // 3
                conv(d['woutb'][j], d['s5_w_out'][j], D, 1024)
                conv(d['wgateb'][j], d['s5_w_gate'][j], D, 1024)
            if l % 3 == 1:
                conv(d['poolwb'].rearrange("g a b -> (g a) b"), d['pool_w'][0].rearrange("g a b -> (g a) b"), 2048, 2048)
            if l % 3 == 2:
                conv(d['wqkvb'], d['attn_w_qkv'][0], D, 512)
                conv(d['wob'], d['attn_w_o'][0], D, 1024)

    def copy_debug(self, src, dst):
        P = self.P
        for s in range(self.nseq):
            for r0 in range(0, L, 512):
                P.dma('sp', DMA(dst[s, r0:r0 + 512], src[s, r0:r0 + 512]), reads=['ys5'], writes=['dst'])

    def s5_prep(self):
        nc, P, d, c = self.nc, self.P, self.d, self.c
        js = sorted(set(l // 3 for l in self.layers if l % 3 == 0))
        with ExitStack() as es:
            sb = lambda n, s, dt=F32: self.sb(es, n, s, dt)
            are, aim, ls = sb('p_are', [128, 64]), sb('p_aim', [128, 64]), sb('p_ls', [128, 1])
            bre, bim = sb('p_bre', [128, 64, 16]), sb('p_bim', [128, 64, 16])
            cre, cim = sb('p_cre', [128, 16, 64]), sb('p_cim', [128, 16, 64])
            bbr, bbi = sb('p_bbr', [128, 64, 16]), sb('p_bbi', [128, 64, 16])
            pwr, pwi = sb('p_pwr', [128, 17, 64]), sb('p_pwi', [128, 17, 64])
            dpr, dpi = sb('p_dpr', [128, NSTEP, 64]), sb('p_dpi', [128, NSTEP, 64])
            tmp = [sb('p_t%d' % i, [128, 1024]) for i in range(4)]
            sm = [sb('p_s%d' % i, [128, 64]) for i in range(8)]
            big = [sb('p_big%d' % i, [128, 16384]) for i in range(1)]
            ss = sb('p_ss', [128, 2, 128])
            s1i = sb('p_qi', [128, 64], mybir.dt.int32)
            T1 = sb('p_T1', [128, 4, NSTEP, 128], BF16)
            T2 = sb('p_T2', [128, 4, NSTEP, 128], BF16)
            bigrr = [0]

            def V(eng, fn, reads, writes):
                P.op(eng, fn, reads=reads, writes=writes)

            def cmul(outr, outi, ar, ai, br, bi, t0, t1, names_in, names_out, eng='dve'):
                V(eng, TT(t0, ar, br, ALU.mult), names_in, ['p_t0'])
                V(eng, TT(t1, ai, bi, ALU.mult), names_in, ['p_t1'])
                V(eng, TT(outr, t0, t1, ALU.subtract), ['p_t0', 'p_t1'], names_out)
                V(eng, TT(t0, ar, bi, ALU.mult), names_in, ['p_t0'])
                V(eng, TT(t1, ai, br, ALU.mult), names_in, ['p_t1'])
                V(eng, TT(outi, t0, t1, ALU.add), ['p_t0', 'p_t1'], names_out)

            for j in js:
                for dr in range(2):
                    jd = j * 2 + dr
                    ld = lambda out, src, nm: P.dma('sp', DMA(out, src), writes=[nm])
                    ld(are[:], d['s5_a_re'][j, dr], 'p_are')
                    ld(aim[:], d['s5_a_im'][j, dr], 'p_aim')
                    P.dma('sp', DMA(ls[:], d['s5_log_step'][j, dr].rearrange("(p o) -> p o", o=1), slow=True), writes=['p_ls'])
                    ld(bre[:], d['s5_b_re'][j, dr], 'p_bre')
                    ld(bim[:], d['s5_b_im'][j, dr], 'p_bim')
                    ld(cre[:], d['s5_c_re'][j, dr], 'p_cre')
                    ld(cim[:], d['s5_c_im'][j, dr], 'p_cim')
                    dt_, zr, zi, mag, magi, cs, sn, tq = sm
                    s0 = tmp[0][:, 0:64]
                    s1 = tmp[1][:, 0:64]
                    V('act', ACTF(dt_[:, 0:1], ls[:], AF.Exp), ['p_ls'], ['p_dt'])
                    V('dve', TS(are[:], are[:], -1e-4, ALU.min), ['p_are'], ['p_are'])
                    V('dve', TS(zr[:], are[:], dt_[:, 0:1], ALU.mult), ['p_are', 'p_dt'], ['p_zr'])
                    V('dve', TS(zi[:], aim[:], dt_[:, 0:1], ALU.mult), ['p_aim', 'p_dt'], ['p_zi'])
                    V('act', ACTF(mag[:], zr[:], AF.Exp), ['p_zr'], ['p_mag'])
                    V('act', ACTF(magi[:], zr[:], AF.Exp, scale=-1.0), ['p_zr'], ['p_magi'])
                    def rred(dst, shift, tname):
                        V('dve', TS(dst, zi[:], shift, ALU.add), ['p_zi'], [tname])
                        V('dve', TS(s1i[:], dst, 1.0 / (2 * math.pi), ALU.mult), [tname], ['p_qi'])
                        V('dve', CP(tq[:], s1i[:]), ['p_qi'], ['p_tq'])
                        V('dve', STT(dst, tq[:], -2 * math.pi, dst, ALU.mult, ALU.add), ['p_tq', tname], [tname])
                        V('dve', TS(tq[:], dst, math.pi, ALU.is_gt), [tname], ['p_tq'])
                        V('dve', STT(dst, tq[:], -2 * math.pi, dst, ALU.mult, ALU.add), ['p_tq', tname], [tname])
                        V('dve', TS(tq[:], dst, -math.pi, ALU.is_lt), [tname], ['p_tq'])
                        V('dve', STT(dst, tq[:], 2 * math.pi, dst, ALU.mult, ALU.add), ['p_tq', tname], [tname])
                    rred(s0, 0.0, 'p_t0')
                    V('act', ACTF(sn[:], s0, AF.Sin), ['p_t0'], ['p_sn'])
                    rred(s1, 0.5 * math.pi, 'p_t1')
                    V('act', ACTF(cs[:], s1, AF.Sin), ['p_t1'], ['p_cs'])
                    K0 = 8
                    V('dve', MEMSET(pwr[:, K0, :], 1.0), [], ['p_pw'])
                    V('dve', MEMSET(pwi[:, K0, :], 0.0), [], ['p_pw'])
                    V('dve', TT(pwr[:, K0 + 1, :], mag[:], cs[:], ALU.mult), ['p_mag', 'p_cs'], ['p_pw'])
                    V('dve', TT(pwi[:, K0 + 1, :], mag[:], sn[:], ALU.mult), ['p_mag', 'p_sn'], ['p_pw'])
                    V('dve', TT(pwr[:, K0 - 1, :], magi[:], cs[:], ALU.mult), ['p_magi', 'p_cs'], ['p_pw'])
                    V('dve', STT(pwi[:, K0 - 1, :], magi[:], -1.0, sn[:], ALU.mult, ALU.mult), ['p_magi', 'p_sn'], ['p_pw'])
                    for k in range(2, 9):
                        cmul(pwr[:, K0 + k, :], pwi[:, K0 + k, :], pwr[:, K0 + k - 1, :], pwi[:, K0 + k - 1, :],
                             pwr[:, K0 + 1, :], pwi[:, K0 + 1, :], s0, s1, ['p_pw'], ['p_pw'])
                        cmul(pwr[:, K0 - k, :], pwi[:, K0 - k, :], pwr[:, K0 - k + 1, :], pwi[:, K0 - k + 1, :],
                             pwr[:, K0 - 1, :], pwi[:, K0 - 1, :], s0, s1, ['p_pw'], ['p_pw'])
                    V('dve', CP(dpr[:, 0, :], pwr[:, K0 + 8, :]), ['p_pw'], ['p_dp'])
                    V('dve', CP(dpi[:, 0, :], pwi[:, K0 + 8, :]), ['p_pw'], ['p_dp'])
                    for i in range(1, NSTEP):
                        cmul(dpr[:, i, :], dpi[:, i, :], dpr[:, i - 1, :], dpi[:, i - 1, :], dpr[:, i - 1, :], dpi[:, i - 1, :],
                             s0, s1, ['p_dp'], ['p_dp'])
                    lr, li = pwr[:, K0 + 1, :], pwi[:, K0 + 1, :]
                    cr_, ci_ = zr, zi
                    V('dve', TS(tq[:], lr, -1.0, ALU.add), ['p_pw'], ['p_tq'])
                    V('dve', TT(s0, are[:], are[:], ALU.mult), ['p_are'], ['p_t0'])
                    V('dve', TT(s1, aim[:], aim[:], ALU.mult), ['p_aim'], ['p_t1'])
                    V('dve', TT(s0, s0, s1, ALU.add), ['p_t0', 'p_t1'], ['p_t0'])
                    V('dve', lambda e, o=mag[:], a=s0: e.reciprocal(out=o, in_=a), ['p_t0'], ['p_mag'])
                    V('dve', TT(s0, tq[:], are[:], ALU.mult), ['p_tq', 'p_are'], ['p_t0'])
                    V('dve', TT(s1, li, aim[:], ALU.mult), ['p_pw', 'p_aim'], ['p_t1'])
                    V('dve', TT(s0, s0, s1, ALU.add), ['p_t0', 'p_t1'], ['p_t0'])
                    V('dve', TT(cr_[:], s0, mag[:], ALU.mult), ['p_t0', 'p_mag'], ['p_zr'])
                    V('dve', TT(s0, li, are[:], ALU.mult), ['p_pw', 'p_are'], ['p_t0'])
                    V('dve', TT(s1, tq[:], aim[:], ALU.mult), ['p_tq', 'p_aim'], ['p_t1'])
                    V('dve', TT(s0, s0, s1, ALU.subtract), ['p_t0', 'p_t1'], ['p_t0'])
                    V('dve', TT(ci_[:], s0, mag[:], ALU.mult), ['p_t0', 'p_mag'], ['p_zi'])
                    crb = cr_[:].unsqueeze(2).broadcast_to([128, 64, 16])
                    cib = ci_[:].unsqueeze(2).broadcast_to([128, 64, 16])
                    t0 = tmp[0][:].rearrange("p (n q) -> p n q", q=16)
                    t1 = tmp[1][:].rearrange("p (n q) -> p n q", q=16)
                    cmul(bbr[:], bbi[:], crb, cib, bre[:], bim[:], t0, t1, ['p_zr', 'p_zi', 'p_bre', 'p_bim'], ['p_bb'])

                    def getbig():
                        return big[0], 'p_big0'

                    def pw(k):
                        return (pwr[:, K0 + k, :], pwi[:, K0 + k, :])
                    engs = ['dve', 'pool']

                    bg, bn = getbig()
                    bv = bg[:].rearrange("p (t q n) -> p t n q", t=8, q=16, n=128)
                    for tau in range(8):
                        k = (7 - tau) if dr == 0 else tau
                        pr, pi = pw(k)
                        prb = pr.unsqueeze(2).broadcast_to([128, 64, 16])
                        pib = pi.unsqueeze(2).broadcast_to([128, 64, 16])
                        eng = engs[tau % 2]
                        ta = tmp[(tau % 2) * 2][:].rearrange("p (n q) -> p n q", q=16)
                        tb = tmp[(tau % 2) * 2 + 1][:].rearrange("p (n q) -> p n q", q=16)
                        na, nb = 'p_t%d' % ((tau % 2) * 2), 'p_t%d' % ((tau % 2) * 2 + 1)
                        V(eng, TT(ta, bbr[:], prb, ALU.mult), ['p_bb', 'p_pw'], [na])
                        V(eng, TT(tb, bbi[:], pib, ALU.mult), ['p_bb', 'p_pw'], [nb])
                        V(eng, TT(bv[:, tau, 0:64, :], ta, tb, ALU.subtract), [na, nb], [bn])
                        V(eng, TT(ta, bbr[:], pib, ALU.mult), ['p_bb', 'p_pw'], [na])
                        V(eng, TT(tb, bbi[:], prb, ALU.mult), ['p_bb', 'p_pw'], [nb])
                        V(eng, TT(bv[:, tau, 64:128, :], ta, tb, ALU.add), [na, nb], [bn])
                    P.dma('sp', DMA(d['rt32'][j, dr], bg[:]), reads=[bn], writes=['rt32'])
                    bg, bn = getbig()
                    bv = bg[:].rearrange("p (n t q) -> p t n q", t=8, q=16, n=128)
                    for tau in range(8):
                        k = -(tau + 1) if dr == 0 else tau - 8
                        pr, pi = pw(k)
                        prb = pr.unsqueeze(2).broadcast_to([128, 64, 16])
                        pib = pi.unsqueeze(2).broadcast_to([128, 64, 16])
                        eng = engs[tau % 2]
                        ta = tmp[(tau % 2) * 2][:].rearrange("p (n q) -> p n q", q=16)
                        tb = tmp[(tau % 2) * 2 + 1][:].rearrange("p (n q) -> p n q", q=16)
                        na, nb = 'p_t%d' % ((tau % 2) * 2), 'p_t%d' % ((tau % 2) * 2 + 1)
                        V(eng, TT(ta, bbr[:], prb, ALU.mult), ['p_bb', 'p_pw'], [na])
                        V(eng, TT(tb, bbi[:], pib, ALU.mult), ['p_bb', 'p_pw'], [nb])
                        V(eng, TT(bv[:, tau, 0:64, :], ta, tb, ALU.subtract), [na, nb], [bn])
                        V(eng, TT(ta, bbr[:], pib, ALU.mult), ['p_bb', 'p_pw'], [na])
                        V(eng, TT(tb, bbi[:], prb, ALU.mult), ['p_bb', 'p_pw'], [nb])
                        V(eng, TT(bv[:, tau, 64:128, :], ta, tb, ALU.add), [na, nb], [bn])
                    P.dma('sp', DMA(d['ry32'][j, dr], bg[:]), reads=[bn], writes=['ry32'])
                    bg, bn = getbig()
                    bv = bg[:].rearrange("p (n t q) -> p t q n", t=8, q=16, n=128)
                    for tau in range(8):
                        k = (tau + 1) if dr == 0 else 8 - tau
                        pr, pi = pw(k)
                        prb = pr.unsqueeze(1).broadcast_to([128, 16, 64])
                        pib = pi.unsqueeze(1).broadcast_to([128, 16, 64])
                        eng = engs[tau % 2]
                        ta = tmp[(tau % 2) * 2][:].rearrange("p (q n) -> p q n", q=16)
                        tb = tmp[(tau % 2) * 2 + 1][:].rearrange("p (q n) -> p q n", q=16)
                        na, nb = 'p_t%d' % ((tau % 2) * 2), 'p_t%d' % ((tau % 2) * 2 + 1)
                        V(eng, TT(ta, cre[:], prb, ALU.mult), ['p_cre', 'p_pw'], [na])
                        V(eng, TT(tb, cim[:], pib, ALU.mult), ['p_cim', 'p_pw'], [nb])
                        V(eng, TT(bv[:, tau, :, 0:64], ta, tb, ALU.subtract), [na, nb], [bn])
                        V(eng, TT(ta, cre[:], pib, ALU.mult), ['p_cre', 'p_pw'], [na])
                        V(eng, TT(tb, cim[:], prb, ALU.mult), ['p_cim', 'p_pw'], [nb])
                        V('dve', STT(bv[:, tau, :, 64:128], ta, -1.0, tb, ALU.mult, ALU.subtract), [na, nb], [bn])
                    P.dma('sp', DMA(d['ot32'][j, dr], bg[:]), reads=[bn], writes=['ot32'])
                    for i in range(NSTEP):
                        V('dve', CP(ss[:, 0, 0:64], dpr[:, i, :]), ['p_dp'], ['p_ss'])
                        V('dve', CP(ss[:, 0, 64:128], dpr[:, i, :]), ['p_dp'], ['p_ss'])
                        V('dve', CP(ss[:, 1, 0:64], dpi[:, i, :]), ['p_dp'], ['p_ss'])
                        V('dve', TS(ss[:, 1, 64:128], dpi[:, i, :], -1.0, ALU.mult), ['p_dp'], ['p_ss'])
                        bk = self.next_bank()
                        ps = self.banks[bk]
                        P.op('pe', [TR(ps[:, 0:128], ss[:, 0, :], c['ident'][:]), TR(ps[:, 128:256], ss[:, 1, :], c['ident'][:])],
                             reads=['p_ss', 'k_ident'], writes=['bank%d' % bk])
                        V('dve', CP(T1[:, jd, i, :], ps[:, 0:128]), ['bank%d' % bk], ['p_T'])
                        V('dve', CP(T2[:, jd, i, :], ps[:, 128:256]), ['bank%d' % bk], ['p_T'])
            for j in js:
                for dr in range(2):
                    P.dma('pool', DMA(d['rt16'][j, dr], d['rt32'][j, dr]), reads=['rt32'], writes=['rt16'])
                    P.dma('pool', DMA(d['ot16'][j, dr], d['ot32'][j, dr]), reads=['ot32'], writes=['ot16'])
            swp32 = sb('p_swap32', [128, 128])
            swp = sb('p_swap', [128, 128], BF16)
            P.dma('sp', DMA(swp32[:], d['c_swap']), writes=['p_swap32'])
            P.op('dve', CP(swp[:], swp32[:]), reads=['p_swap32'], writes=['p_swap'])
            pmt = [sb('p_pmt%d' % i, [128, NSTEP, 128], BF16) for i in range(2)]
            pa = [sb('p_pa%d' % i, [128, NSTEP, 128], BF16) for i in range(2)]
            pb = [sb('p_pb%d' % i, [128, NSTEP, 128], BF16) for i in range(2)]
            idb = c['identb'][:].unsqueeze(1).broadcast_to([128, NSTEP, 128])
            swb = swp[:].unsqueeze(1).broadcast_to([128, NSTEP, 128])
            cnt = 0
            for j in js:
                for dr in range(2):
                    jd = j * 2 + dr
                    for g in range(128):
                        u = cnt % 2
                        cnt += 1
                        e1 = 'dve' if u == 0 else 'pool'
                        t1b = T1[:, jd, :, g:g + 1].broadcast_to([128, NSTEP, 128])
                        t2b = T2[:, jd, :, g:g + 1].broadcast_to([128, NSTEP, 128])
                        V(e1, TT(pa[u][:], idb, t1b, ALU.mult), ['p_T', 'k_identb'], ['p_pa%d' % u])
                        V(e1, TT(pb[u][:], swb, t2b, ALU.mult), ['p_T', 'p_swap'], ['p_pb%d' % u])
                        V(e1, TT(pmt[u][:], pa[u][:], pb[u][:], ALU.add), ['p_pa%d' % u, 'p_pb%d' % u], ['p_pmt%d' % u])
                        P.dma('sp', DMA(d['pm16'][j, dr, g], pmt[u][:].rearrange("p a b -> p (a b)")),
                              reads=['p_pmt%d' % u], writes=['pm16'])
            mkf, mkb = sb('p_mkf', [128, 128]), sb('p_mkb', [128, 128])
            sel, dsk, DS = sb('p_sel', [16, 128]), sb('p_dsk', [16, 2, 128]), sb('p_DS', [128, 2, 128])
            P.dma('sp', DMA(mkf[:], d['c_maskf']), writes=['p_mkf'])
            P.dma('sp', DMA(mkb[:], d['c_maskb']), writes=['p_mkb'])
            P.dma('sp', DMA(sel[:], d['c_sel']), writes=['p_sel'])
            for j in js:
                P.dma('sp', DMA(dsk[:, j, :], d['s5_d'][j].rearrange("(g q) -> q g", q=16), slow=True), writes=['p_dsk'])
                bk = self.next_bank()
                ps = self.banks[bk]
                P.op('pe', MM(ps[:, 0:128], sel[:], dsk[:, j, :]), reads=['p_sel', 'p_dsk'], writes=['bank%d' % bk])
                V('dve', CP(DS[:, j, :], ps[:, 0:128]), ['bank%d' % bk], ['p_DS'])
            NMB = 6
            wl = [sb('p_wl%d' % i, [128, 4, 128]) for i in range(NMB)]
            m1 = [sb('p_m1%d' % i, [128, 128]) for i in range(NMB)]
            m2 = [sb('p_m2%d' % i, [128, 128]) for i in range(NMB)]
            mo = [sb('p_mo%d' % i, [128, 128], BF16) for i in range(NMB)]
            cnt = 0
            for j in js:
                for g in range(128):
                    u = cnt % NMB
                    cnt += 1
                    nm = 'p_wl%d' % u
                    for dr in range(2):
                        P.dma('sp', DMA(wl[u][:, dr * 2, :], d['ry32'][j, dr, g].rearrange("(n k) -> n k", k=128)),
                              reads=['ry32'], writes=[nm])
                        P.dma('sp', DMA(wl[u][:, dr * 2 + 1, :], d['ot32'][j, dr, g].rearrange("(n k) -> n k", k=128)),
                              reads=['ot32'], writes=[nm])
                    bk = self.next_bank()
                    ps = self.banks[bk]
                    P.op('pe', [MM(ps[:, 0:128], wl[u][:, 0, :], wl[u][:, 1, :]), MM(ps[:, 128:256], wl[u][:, 2, :], wl[u][:, 3, :])],
                         reads=[nm], writes=['bank%d' % bk])
                    V('dve', TT(m1[u][:], ps[:, 0:128], mkf[:], ALU.mult), ['bank%d' % bk, 'p_mkf'], ['p_m1%d' % u])
                    V('dve', TT(m2[u][:], ps[:, 128:256], mkb[:], ALU.mult), ['bank%d' % bk, 'p_mkb'], ['p_m2%d' % u])
                    V('pool', TT(m1[u][:], m1[u][:], m2[u][:], ALU.add), ['p_m1%d' % u, 'p_m2%d' % u], ['p_m1%d' % u])
                    V('dve', STT(mo[u][:], c['ident'][:], DS[:, j, g:g + 1], m1[u][:], ALU.mult, ALU.add),
                      ['p_m1%d' % u, 'p_DS', 'k_ident'], ['p_mo%d' % u])
                    P.dma('act', DMA(d['mt16'][j, g], mo[u][:]), reads=['p_mo%d' % u], writes=['mt16'])
            P.barrier()
            P.flush()

    def s5_scan(self, j, src):
        nc, P, d, c = self.nc, self.P, self.d, self.c
        NG = 2
        with ExitStack() as es:
            sb = lambda n, s, dt=F32: self.sb(es, n, s, dt)
            xin2 = [sb('s_xin%d' % i, [128, 4, 8, 128]) for i in range(2)]
            xb = sb('s_xb', [128, 4, 8, 128], BF16)
            U = sb('s_U', [128, 8, 512], BF16)
            yt = sb('s_yt', [128, 4, 8, 128])
            S = [[sb('s_S%d%d' % (u, dr), [128, 512], BF16) for dr in range(2)] for u in range(NG)]
            tmpc = [sb('s_tc%d' % u, [128, 512]) for u in range(NG)]
            WA2 = [[sb('s_WA%d_%d' % (v, u), [128, 5, 128], BF16) for u in range(NG)] for v in range(2)]
            WP2 = [[sb('s_WP%d_%d' % (v, u), [128, 2, NSTEP, 128], BF16) for u in range(NG)] for v in range(2)]

            def load_w(pidx):
                v = pidx % 2
                cbp, gpp = divmod(pidx, 8 // NG)
                for u in range(NG):
                    g = cbp * 8 + gpp * NG + u
                    nm = 's_W%d_%d' % (v, u)
                    WAu, WPu = WA2[v][u], WP2[v][u]
                    P.dma('sp', DMA(WAu[:, 0, :], d['rt16'][j, 0, g].rearrange("(a b) -> a b", b=128)), reads=['rt16'], writes=[nm])
                    P.dma('sp', DMA(WAu[:, 1, :], d['rt16'][j, 1, g].rearrange("(a b) -> a b", b=128)), reads=['rt16'], writes=[nm])
                    P.dma('sp', DMA(WAu[:, 2, :], d['ot16'][j, 0, g].rearrange("(a b) -> a b", b=128)), reads=['ot16'], writes=[nm])
                    P.dma('sp', DMA(WAu[:, 3, :], d['ot16'][j, 1, g].rearrange("(a b) -> a b", b=128)), reads=['ot16'], writes=[nm])
                    P.dma('sp', DMA(WAu[:, 4, :], d['mt16'][j, g]), reads=['mt16'], writes=[nm])
                    for dr in range(2):
                        P.dma('sp', DMA(WPu[:, dr].rearrange("p a b -> p (a b)"), d['pm16'][j, dr, g]), reads=['pm16'], writes=[nm])
            NPAIR = 16 * (8 // NG)
            for s in range(self.nseq):
                xv = src[s].rearrange("(ct c t) (cb ch) -> cb c ct t ch", c=128, t=8, ch=128)
                yv = d['ys5'][s].rearrange("(ct c t) (cb ch) -> cb c ct t ch", c=128, t=8, ch=128)
                def load_x(cb, xv=xv):
                    for ct in range(4):
                        P.dma('sp', DMA(xin2[cb % 2][:, ct], xv[cb][:, ct]), reads=['src'], writes=['s_xin%d' % (cb % 2)])
                load_x(0)
                load_w(0)
                for cb in range(16):
                    xin = xin2[cb % 2]
                    if cb + 1 < 16:
                        load_x(cb + 1)
                    for ct in range(4):
                        P.op('act', ACTF(xb[:, ct].rearrange("p g (t q) -> p t g q", q=16),
                                         xin[:, ct].rearrange("p t (g q) -> p t g q", q=16), AF.Copy), reads=['s_xin%d' % (cb % 2)], writes=['s_xb'])
                    for g8 in range(8):
                        fns = []
                        for ct in range(4):
                            fns.append(TR(self.bank16[:, ct * 128:(ct + 1) * 128], xb[:, ct, g8, :], c['identb'][:]))
                        P.op('pe', fns, reads=['s_xb', 'k_identb'], writes=['bank16'])
                        if g8 % 2 == 0:
                            P.op('dve', CP(U[:, g8, :], self.bank16[:, 0:512]), reads=['bank16'], writes=['s_U%d' % g8])
                        else:
                            P.op('act', ACTF(U[:, g8, :], self.bank16[:, 0:512], AF.Copy), reads=['bank16'], writes=['s_U%d' % g8])
                    for gp in range(0, 8, NG):
                        pidx = cb * (8 // NG) + gp // NG
                        if pidx + 1 < NPAIR:
                            load_w(pidx + 1)
                        WA, WP = WA2[pidx % 2], WP2[pidx % 2]
                        wnm = ['s_W%d_%d' % (pidx % 2, u) for u in range(NG)]
                        chains = [(u, dr) for u in range(NG) for dr in range(2)]
                        cbank = {}
                        for (u, dr) in chains:
                            bk = self.next_bank()
                            cbank[(u, dr)] = bk
                            ps = self.banks[bk]
                            sn_ = 's_S%d%d' % (u, dr)
                            P.op('pe', MM(ps[:, :], WA[u][:, dr, :], U[:, gp + u, :]), reads=[wnm[u], 's_U%d' % (gp + u)], writes=['bank%d' % bk])
                            if dr == 0:
                                P.op('dve', [MEMSET(S[u][dr][:, 0:1], 0.0), CP(S[u][dr][:, 1:512], ps[:, 0:511])], reads=['bank%d' % bk], writes=[sn_])
                            else:
                                P.op('dve', [MEMSET(S[u][dr][:, 511:512], 0.0), CP(S[u][dr][:, 0:511], ps[:, 1:512])], reads=['bank%d' % bk], writes=[sn_])
                        for i in range(NSTEP):
                            sh = 1 << i
                            for (u, dr) in chains:
                                bk = cbank[(u, dr)]
                                ps = self.banks[bk]
                                sn_ = 's_S%d%d' % (u, dr)
                                if dr == 0:
                                    o, r = ps[:, sh:512], S[u][dr][:, 0:512 - sh]
                                else:
                                    o, r = ps[:, 0:512 - sh], S[u][dr][:, sh:512]
                                P.op('pe', MM(o, WP[u][:, dr, i, :], r), reads=[wnm[u], sn_], writes=['bank%d' % bk])
                            for (u, dr) in chains:
                                bk = cbank[(u, dr)]
                                ps = self.banks[bk]
                                sn_ = 's_S%d%d' % (u, dr)
                                if dr == 0:
                                    o, a = S[u][dr][:, sh:512], ps[:, sh:512]
                                else:
                                    o, a = S[u][dr][:, 0:512 - sh], ps[:, 0:512 - sh]
                                P.op('dve', TT(o, a, o, ALU.add), reads=['bank%d' % bk, sn_], writes=[sn_])
                        for u in range(NG):
                            g8 = gp + u
                            bk = self.next_bank()
                            ps = self.banks[bk]
                            fns = []
                            for ct in range(4):
                                o = ps[:, ct * 128:(ct + 1) * 128]
                                fns.append(MM(o, U[:, g8, ct * 128:(ct + 1) * 128], WA[u][:, 4, :], True, False))
                                fns.append(MM(o, S[u][0][:, ct * 128:(ct + 1) * 128], WA[u][:, 2, :], False, False))
                                fns.append(MM(o, S[u][1][:, ct * 128:(ct + 1) * 128], WA[u][:, 3, :], False, True))
                            P.op('pe', fns, reads=[wnm[u], 's_U%d' % g8, 's_S%d0' % u, 's_S%d1' % u], writes=['bank%d' % bk])
                            P.op('act', ACTF(yt[:, :, :, g8 * 16:(g8 + 1) * 16],
                                             ps[:, :].rearrange("p (a t q) -> p a t q", a=4, t=8, q=16), AF.Copy),
                                 reads=['bank%d' % bk], writes=['s_yt'])
                    for ct in range(4):
                        P.dma('sp', DMA(yv[cb][:, ct], yt[:, ct]), reads=['s_yt'], writes=['ys5'])
            P.barrier()
            P.flush()

    def kv_phase(self, src):
        nc, P, d, c = self.nc, self.P, self.d, self.c
        with ExitStack() as es:
            sb = lambda n, s, dt=F32: self.sb(es, n, s, dt)
            tm = [sb('v_tm%d' % i, [128, D]) for i in range(2)]
            tmb = [sb('v_tmb%d' % i, [128, D], BF16) for i in range(2)]
            Xb = sb('v_Xb', [128, NKC, BLK], BF16)
            W = [sb('v_W%d' % i, [128, NKC, 512], BF16) for i in range(2)]
            kq = sb('v_kq', [128, BLK])
            ksq = sb('v_ksq', [128, BLK])
            rs = sb('v_rs', [128, BLK])
            kh = sb('v_kh', [128, BLK])
            ta = sb('v_ta', [128, BLK])
            ktb = sb('v_ktb', [128, 4, BLK], BF16)
            vt = sb('v_vt', [128, 4, 512], BF16)
            cs, sn = sb('v_cos', [128, BLK]), sb('v_sin', [128, BLK])
            wv = d['wqkvb'].rearrange("(k p) n -> p k n", p=128)
            P.dma('sp', DMA(W[0][:], wv[:, :, 2048:2560]), reads=['wscr'], writes=['v_W0'])
            P.dma('sp', DMA(W[1][:], wv[:, :, 2560:3072]), reads=['wscr'], writes=['v_W1'])
            tcount = 0
            for s in range(self.nseq):
                for b in range(self.nblk):
                    t0 = b * BLK
                    P.dma('sp', DMA(cs[:], d['c_cos'][:, t0:t0 + BLK]), writes=['v_cos'])
                    P.dma('sp', DMA(sn[:], d['c_sin'][:, t0:t0 + BLK]), writes=['v_sin'])
                    for tt in range(4):
                        u = tcount % 2
                        tcount += 1
                        P.dma('sp', DMA(tm[u][:], src[s, t0 + tt * 128:t0 + (tt + 1) * 128, :]), reads=['src'], writes=['v_tm%d' % u])
                        P.op('act', ACTF(tmb[u][:], tm[u][:], AF.Copy), reads=['v_tm%d' % u], writes=['v_tmb%d' % u])
                        for k4 in range(4):
                            fns = [TR(self.bank16[:, i * 128:(i + 1) * 128], tmb[u][:, (k4 * 4 + i) * 128:(k4 * 4 + i + 1) * 128], c['identb'][:])
                                   for i in range(4)]
                            P.op('pe', fns, reads=['v_tmb%d' % u, 'k_identb'], writes=['bank16'])
                            P.op('dve', CP(Xb[:, k4 * 4:(k4 + 1) * 4, tt * 128:(tt + 1) * 128],
                                           self.bank16[:, 0:512].rearrange("p (a b) -> p a b", a=4)),
                                 reads=['bank16'], writes=['v_Xb'])
                    for h in range(4):
                        bk = self.next_bank()
                        ps = self.banks[bk]
                        P.op('pe', [MM(ps[:, :], W[0][:, kc, h * 128:(h + 1) * 128], Xb[:, kc, :], kc == 0, kc == NKC - 1) for kc in range(NKC)],
                             reads=['v_W0', 'v_Xb'], writes=['bank%d' % bk])
                        self.qk_norm_rope(ps, bk, ktb[:, h, :], 'v_ktb', 1, kq, ksq, rs, kh, ta, cs, sn, 'v')
                    for h in range(4):
                        P.dma('sp', DMA(d['kts'][s, h, :, t0:t0 + BLK], ktb[:, h, :]), reads=['v_ktb'], writes=['kts'])
                    for tt in range(4):
                        bk = self.next_bank()
                        ps = self.banks[bk]
                        P.op('pe', [MM(ps[:, :], Xb[:, kc, tt * 128:(tt + 1) * 128], W[1][:, kc, :], kc == 0, kc == NKC - 1) for kc in range(NKC)],
                             reads=['v_W1', 'v_Xb'], writes=['bank%d' % bk])
                        P.op('act', ACTF(vt[:, tt, :], ps[:, :], AF.Copy),
                             reads=['bank%d' % bk], writes=['v_vt'])
                    for tt in range(4):
                        sc = b * 4 + tt
                        P.dma('sp', DMA(d['vs'][s, :, :, sc, :].rearrange("h p e -> p h e"),
                                        vt[:, tt, :].rearrange("p (h e) -> p h e", h=4)), reads=['v_vt'], writes=['vs'])
            P.barrier()
            P.flush()

    def qk_norm_rope(self, ps, bk, out_bf, out_name, which, kq, ksq, rs, kh, ta, cs, sn, pfx):
        P, c = self.P, self.c
        bn = 'bank%d' % bk
        n = lambda x: pfx + '_' + x
        P.op('act', ACTF(kq[:], ps[:, :], AF.Copy), reads=[bn], writes=[n('kq')])
        P.op('pool', TT(ksq[:], kq[:], kq[:], ALU.mult), reads=[n('kq')], writes=[n('ksq')])
        b2 = self.next_bank()
        ps2 = self.banks[b2]
        P.op('pe', MM(ps2[:, :], c['ones_rms'][:], ksq[:]), reads=[n('ksq'), 'k_ones_rms'], writes=['bank%d' % b2])
        P.op('act', ACTF(rs[:], ps2[:, :], AF.Sqrt, bias=c['eps'][:, 1:2]), reads=['bank%d' % b2, 'k_eps'], writes=[n('rs')])
        P.op('dve', lambda e, o=rs[:]: e.reciprocal(out=o, in_=o), reads=[n('rs')], writes=[n('rs')])
        P.op('dve', STT(kh[:], kq[:], c['qkn'][:, which:which + 1], rs[:], ALU.mult, ALU.mult), reads=[n('kq'), n('rs'), 'k_qkn'], writes=[n('kh')])
        b3 = self.next_bank()
        ps3 = self.banks[b3]
        P.op('pe', MM(ps3[:, :], c['rot'][:], kh[:]), reads=[n('kh'), 'k_rot'], writes=['bank%d' % b3])
        P.op('pool', TT(ta[:], kh[:], cs[:], ALU.mult), reads=[n('kh'), n('cos')], writes=[n('ta')])
        P.op('dve', TT(rs[:], ps3[:, :], sn[:], ALU.mult), reads=['bank%d' % b3, n('sin')], writes=[n('rs')])
        P.op('pool', TT(out_bf, ta[:], rs[:], ALU.add), reads=[n('ta'), n('rs')], writes=[out_name])

    def block_phase(self, l, src, dst):
        nc, P, d, c = self.nc, self.P, self.d, self.c
        kind = l % 3
        j = l // 3
        with ExitStack() as es:
            sb = lambda n, s, dt=F32: self.sb(es, n, s, dt)
            X = sb('b_X', [128, NKC, XW])
            Xb = sb('b_Xb', [128, NKC, BLK], BF16)
            HT = sb('b_HT', [128, 64, BLK], BF16)
            tm = [sb('b_tm%d' % i, [128, D]) for i in range(2)]
            W = [sb('b_W%d' % i, [128, NKC, 512], BF16) for i in range(2)]
            st = [sb('b_st%d' % i, [128, BLK]) for i in range(6)]
            sq = [sb('b_sq%d' % i, [128, BLK]) for i in range(2)]
            sqb = [sb('b_sqb%d' % i, [128, BLK], BF16) for i in range(2)]
            pt = [sb('b_pt%d' % i, [128, BLK], BF16) for i in range(8)]
            self._wrr = 0
            self._tmrr = 0
            self._sqrr = 0
            self._ptrr = 0
            XM = lambda kc: X[:, kc, PAD:PAD + BLK]

            def wload(view_fn):
                u = self._wrr
                self._wrr ^= 1
                view_fn(W[u], 'b_W%d' % u)
                return W[u], 'b_W%d' % u

            def mm16(lhs_fn, rhs_fn, reads, nk=NKC):
                bk = self.next_bank()
                ps = self.banks[bk]
                P.op('pe', [MM(ps[:, :], lhs_fn(kc), rhs_fn(kc), kc == 0, kc == nk - 1) for kc in range(nk)],
                     reads=reads, writes=['bank%d' % bk])
                return ps, 'bank%d' % bk

            def layer_norm(which):
                g_ap = lambda kc: c['lnp'][:, which * 2, l, kc:kc + 1]
                b_ap = lambda kc: c['lnp'][:, which * 2 + 1, l, kc:kc + 1]
                bm = self.next_bank()
                bq = self.next_bank()
                psm, psq = self.banks[bm], self.banks[bq]
                for kc in range(NKC):
                    u = self._sqrr
                    self._sqrr ^= 1
                    P.op('pool', TT(sqb[u][:], XM(kc), XM(kc), ALU.mult), reads=['b_X%d' % kc], writes=['b_sqb%d' % u])
                    P.op('pe', [MM(psm[:, :], c['ones_ln'][:], XM(kc), kc == 0, kc == NKC - 1),
                                MM(psq[:, :], c['onesb'][:], sqb[u][:], kc == 0, kc == NKC - 1)],
                         reads=['b_X%d' % kc, 'b_sqb%d' % u, 'k_ones_ln', 'k_onesb'], writes=['bank%d' % bm, 'bank%d' % bq])
                mean, var, rstd, nmr = st[0], st[1], st[2], st[3]
                P.op('act', ACTF(mean[:], psm[:, :], AF.Copy), reads=['bank%d' % bm], writes=['b_st0'])
                P.op('dve', TT(var[:], mean[:], mean[:], ALU.mult), reads=['b_st0'], writes=['b_st1'])
                P.op('dve', STT(var[:], psq[:, :], 1.0 / D, var[:], ALU.mult, ALU.subtract), reads=['bank%d' % bq, 'b_st1'], writes=['b_st1'])
                P.op('act', ACTF(rstd[:], var[:], AF.Sqrt, bias=c['eps'][:, 0:1]), reads=['b_st1', 'k_eps'], writes=['b_st2'])
                P.op('dve', lambda e, o=rstd[:]: e.reciprocal(out=o, in_=o), reads=['b_st2'], writes=['b_st2'])
                P.op('dve', STT(nmr[:], mean[:], -1.0, rstd[:], ALU.mult, ALU.mult), reads=['b_st0', 'b_st2'], writes=['b_st3'])
                for kc in range(NKC):
                    u = self._sqrr
                    self._sqrr ^= 1
                    e1 = 'dve' if kc % 2 == 0 else 'pool'
                    P.op(e1, TT(sq[u][:], XM(kc), rstd[:], ALU.mult), reads=['b_X%d' % kc, 'b_st2'], writes=['b_sq%d' % u])
                    P.op(e1, TT(sq[u][:], sq[u][:], nmr[:], ALU.add), reads=['b_sq%d' % u, 'b_st3'], writes=['b_sq%d' % u])
                    P.op('act', [ACTF(XM(kc), sq[u][:], AF.Identity, scale=g_ap(kc), bias=b_ap(kc)),
                                 ACTF(Xb[:, kc, :], sq[u][:], AF.Identity, scale=g_ap(kc), bias=b_ap(kc))],
                         reads=['b_sq%d' % u, 'k_lnp'], writes=['b_X%d' % kc, 'b_Xb%d' % kc])

            dummy = sb('b_dummy', [128, 2])
            self.pool_t = [[sb('b_pl%d%d' % (e_, i_), [128, XW]) for i_ in range(2)] for e_ in range(2)] if kind == 1 else None
            self.st2 = [sb('b_sx%d' % i_, [128, BLK]) for i_ in range(5)] if kind == 2 else None
            self.dacc = [sb('b_dacc%d' % i_, [128, BLK]) for i_ in range(2)] if kind == 2 else None
            if kind == 2:
                self.onesbs = sb('b_onesbs', [128, 128], BF16)
                P.op('dve', TS(self.onesbs[:], c['onesb'][:], 1.0 / 128, ALU.mult), reads=['k_onesb'], writes=['b_onesbs'])
            fence_names = ['b_HT0', 'b_HT1', 'b_HT2', 'b_HT3', 'b_OT', 'b_ga', 'a_cos', 'a_sin'] + ['b_QT%d' % h for h in range(16)]

            def fence():
                P.op('pool', MEMSET(dummy[:, 0:1], 0.0), reads=[], writes=fence_names)

            XR = ['b_X%d' % kc for kc in range(NKC)]
            XBR = ['b_Xb%d' % kc for kc in range(NKC)]

            for s in range(self.nseq):
                for b in range(self.nblk):
                    t0 = b * BLK
                    for tt in range(4):
                        u = self._tmrr
                        self._tmrr ^= 1
                        P.dma('sp', DMA(tm[u][:], src[s, t0 + tt * 128:t0 + (tt + 1) * 128, :]), reads=['src'], writes=['b_tm%d' % u])
                        for k4 in range(4):
                            bk = self.next_bank()
                            ps = self.banks[bk]
                            P.op('pe', [TR(ps[:, i * 128:(i + 1) * 128], tm[u][:, (k4 * 4 + i) * 128:(k4 * 4 + i + 1) * 128], c['ident'][:])
                                        for i in range(4)], reads=['b_tm%d' % u, 'k_ident'], writes=['bank%d' % bk])
                            P.op('dve' if k4 % 2 == 0 else 'act',
                                 CP(X[:, k4 * 4:(k4 + 1) * 4, PAD + tt * 128:PAD + (tt + 1) * 128], ps[:, :].rearrange("p (a b) -> p a b", a=4))
                                 if k4 % 2 == 0 else
                                 ACTF(X[:, k4 * 4:(k4 + 1) * 4, PAD + tt * 128:PAD + (tt + 1) * 128], ps[:, :].rearrange("p (a b) -> p a b", a=4), AF.Copy),
                                 reads=['bank%d' % bk], writes=XR[k4 * 4:(k4 + 1) * 4])
                    fence()
                    if kind == 0:
                        self.mixer_s5(j, s, t0, X, XM, Xb, HT, tm, W, st, wload, mm16, XR)
                    elif kind == 1:
                        self.mixer_pool(s, t0, src, X, XM, HT, tm, W, st, sq, wload, mm16, XR)
                    else:
                        self.mixer_attn(s, t0, X, XM, Xb, HT, W, st, pt, wload, mm16, XR, XBR)
                    layer_norm(0)
                    fence()
                    w1v = d['w1b'][l].rearrange("(k p) f -> p k f", p=128)
                    for fg in range(16):
                        Wt, wn = wload(lambda buf, nm, fg=fg: P.dma('sp', DMA(buf[:], w1v[:, :, fg * 512:(fg + 1) * 512]), reads=['wscr'], writes=[nm]))
                        for fq in range(4):
                            ps, bn = mm16(lambda kc, Wt=Wt, fq=fq: Wt[:, kc, fq * 128:(fq + 1) * 128], lambda kc: Xb[:, kc, :], [wn] + XBR)
                            u = self._sqrr
                            self._sqrr ^= 1
                            P.op('act', ACTF(sq[u][:], ps[:, :], AF.Relu), reads=[bn], writes=['b_sq%d' % u])
                            fidx = fg * 4 + fq
                            P.op('dve' if fq % 2 == 0 else 'pool', TT(HT[:, fidx, :], sq[u][:], sq[u][:], ALU.mult),
                                 reads=['b_sq%d' % u], writes=['b_HT%d' % (fidx // 16)])
                    w2v = d['w2b'][l].rearrange("(k p) n -> p k n", p=128)
                    for dg in range(4):
                        for fs in range(4):
                            Wt, wn = wload(lambda buf, nm, dg=dg, fs=fs: P.dma('sp', DMA(buf[:], w2v[:, fs * 16:(fs + 1) * 16, dg * 512:(dg + 1) * 512]),
                                                                              reads=['wscr'], writes=[nm]))
                            for dq in range(4):
                                kc_o = dg * 4 + dq
                                ps, bn = mm16(lambda fc, Wt=Wt, dq=dq: Wt[:, fc, dq * 128:(dq + 1) * 128],
                                              lambda fc, fs=fs: HT[:, fs * 16 + fc, :], [wn, 'b_HT%d' % fs])
                                if fs == 0:
                                    P.op('dve', STT(XM(kc_o), XM(kc_o), ALPHA, ps[:, :], ALU.mult, ALU.add), reads=[bn, XR[kc_o]], writes=[XR[kc_o]])
                                else:
                                    P.op('dve', TT(XM(kc_o), XM(kc_o), ps[:, :], ALU.add), reads=[bn, XR[kc_o]], writes=[XR[kc_o]])
                    layer_norm(1)
                    for tt in range(4):
                        u = self._tmrr
                        self._tmrr ^= 1
                        for k4 in range(4):
                            bk = self.next_bank()
                            ps = self.banks[bk]
                            P.op('pe', [TR(ps[:, i * 128:(i + 1) * 128], X[:, k4 * 4 + i, PAD + tt * 128:PAD + (tt + 1) * 128], c['ident'][:])
                                        for i in range(4)], reads=XR[k4 * 4:(k4 + 1) * 4] + ['k_ident'], writes=['bank%d' % bk])
                            if k4 % 2 == 0:
                                P.op('dve', CP(tm[u][:, k4 * 512:(k4 + 1) * 512], ps[:, :]), reads=['bank%d' % bk], writes=['b_tm%d' % u])
                            else:
                                P.op('act', ACTF(tm[u][:, k4 * 512:(k4 + 1) * 512], ps[:, :], AF.Copy), reads=['bank%d' % bk], writes=['b_tm%d' % u])
                        P.dma('pool', DMA(dst[s, t0 + tt * 128:t0 + (tt + 1) * 128, :], tm[u][:]), reads=['b_tm%d' % u], writes=['dst'])
            P.barrier()
            P.flush()

    def mixer_s5(self, j, s, t0, X, XM, Xb, HT, tm, W, st, wload, mm16, XR):
        P, d, c = self.P, self.d, self.c
        GT = HT
        ga, gb_ = HT[:, 32:36, :].rearrange("p a b -> p (a b)"), HT[:, 36:40, :].rearrange("p a b -> p (a b)")
        for tt in range(4):
            u = self._tmrr
            self._tmrr ^= 1
            y = tm[u]
            yn = 'b_tm%d' % u
            P.dma('sp', DMA(y[:], d['ys5'][s, t0 + tt * 128:t0 + (tt + 1) * 128, :]), reads=['ys5'], writes=[yn])
            u2 = self._tmrr
            tq = tm[u2]
            tn = 'b_tm%d' % u2
            P.op('pool', TT(tq[:], y[:], y[:], ALU.mult), reads=[yn], writes=[tn])
            P.op('pool', TS(tq[:], tq[:], 0.044715, ALU.mult, 1.0, ALU.add), reads=[tn], writes=[tn])
            P.op('dve', TT(tq[:], tq[:], y[:], ALU.mult), reads=[tn, yn], writes=[tn])
            P.op('act', ACTF(tq[:], tq[:], AF.Sigmoid, scale=GC), reads=[tn], writes=[tn])
            P.op('dve', TT(ga, tq[:], y[:], ALU.mult), reads=[tn, yn], writes=['b_ga'])
            for k4 in range(4):
                P.op('pe', [TR(self.bank16[:, i * 128:(i + 1) * 128], ga[:, (k4 * 4 + i) * 128:(k4 * 4 + i + 1) * 128], c['identb'][:]) for i in range(4)],
                     reads=['b_ga', 'k_identb'], writes=['bank16'])
                P.op('dve', CP(GT[:, k4 * 4:(k4 + 1) * 4, tt * 128:(tt + 1) * 128], self.bank16[:, 0:512].rearrange("p (a b) -> p a b", a=4)),
                     reads=['bank16'], writes=['b_HT0'])
        wo = d['woutb'][j].rearrange("(k p) n -> p k n", p=128)
        wg = d['wgateb'][j].rearrange("(k p) n -> p k n", p=128)
        for dg in range(4):
            Wo, won = wload(lambda buf, nm, dg=dg: P.dma('sp', DMA(buf[:], wo[:, :, dg * 512:(dg + 1) * 512]), reads=['wscr'], writes=[nm]))
            Wg, wgn = wload(lambda buf, nm, dg=dg: P.dma('sp', DMA(buf[:], wg[:, :, dg * 512:(dg + 1) * 512]), reads=['wscr'], writes=[nm]))
            for dq in range(4):
                kc_o = dg * 4 + dq
                pso, bo = mm16(lambda kc, Wo=Wo, dq=dq: Wo[:, kc, dq * 128:(dq + 1) * 128], lambda kc: GT[:, kc, :], [won, 'b_HT0'])
                psg, bg = mm16(lambda kc, Wg=Wg, dq=dq: Wg[:, kc, dq * 128:(dq + 1) * 128], lambda kc: GT[:, kc, :], [wgn, 'b_HT0'])
                sg = st[4 + (dq % 2)]
                sgn = 'b_st%d' % (4 + (dq % 2))
                P.op('act', ACTF(sg[:], psg[:, :], AF.Sigmoid), reads=[bg], writes=[sgn])
                P.op('dve', TT(sg[:], pso[:, :], sg[:], ALU.mult), reads=[bo, sgn], writes=[sgn])
                P.op('dve', STT(XM(kc_o), XM(kc_o), ALPHA, sg[:], ALU.mult, ALU.add), reads=[sgn, XR[kc_o]], writes=[XR[kc_o]])

    def mixer_pool(self, s, t0, src, X, XM, HT, tm, W, st, sq, wload, mm16, XR):
        P, d, c = self.P, self.d, self.c
        u = self._tmrr
        self._tmrr ^= 1
        hb = tm[u]
        hn = 'b_tm%d' % u
        P.op('pool', MEMSET(hb[0:16, :], 0.0), reads=[], writes=[hn])
        if t0 > 0:
            P.dma('sp', DMA(hb[0:8, :], src[s, t0 - 8:t0, :]), reads=['src'], writes=[hn])
        if t0 + BLK < L:
            P.dma('sp', DMA(hb[8:16, :], src[s, t0 + BLK:t0 + BLK + 8, :]), reads=['src'], writes=[hn])
        for k4 in range(4):
            bk = self.next_bank()
            ps = self.banks[bk]
            P.op('pe', [TR(ps[:, i * 16:(i + 1) * 16], hb[0:16, (k4 * 4 + i) * 128:(k4 * 4 + i + 1) * 128], c['ident'][0:16, 0:16]) for i in range(4)],
                 reads=[hn, 'k_ident'], writes=['bank%d' % bk])
            pv = ps[:, 0:64].rearrange("p (a b) -> p a b", a=4)
            P.op('dve', [CP(X[:, k4 * 4:(k4 + 1) * 4, 0:PAD], pv[:, :, 0:8]), CP(X[:, k4 * 4:(k4 + 1) * 4, PAD + BLK:XW], pv[:, :, 8:16])],
                 reads=['bank%d' % bk], writes=XR[k4 * 4:(k4 + 1) * 4])
        ic = tm[self._tmrr]
        icn = 'b_tm%d' % self._tmrr
        self._tmrr ^= 1
        P.dma('sp', DMA(ic[:, 0:4 * BLK].rearrange("p (a b) -> p a b", a=4), d['c_invc'][:, :, t0:t0 + BLK]), writes=[icn])
        PT = HT
        T = [st[0], st[1]]
        for kc in range(NKC):
            gi = kc // 4
            eng = 'dve' if kc % 2 == 0 else 'pool'
            a = [st[(kc % 2) * 2], st[(kc % 2) * 2 + 1]]
            an = ['b_st%d' % ((kc % 2) * 2), 'b_st%d' % ((kc % 2) * 2 + 1)]
            xs = X[:, kc, :]
            w = (2, 4, 8, 16)[gi]
            j0 = PAD - w // 2
            e_i = kc % 2
            ta_, tb_ = self.pool_t[e_i][0], self.pool_t[e_i][1]
            tan, tbn = 'b_pl%d0' % e_i, 'b_pl%d1' % e_i
            acc, accn = a[0], an[0]
            if w == 2:
                P.op(eng, TT(acc[:], xs[:, j0:j0 + BLK], xs[:, j0 + 1:j0 + 1 + BLK], ALU.add), reads=[XR[kc]], writes=[accn])
            else:
                P.op(eng, TT(ta_[:, 0:XW - 1], xs[:, 0:XW - 1], xs[:, 1:XW], ALU.add), reads=[XR[kc]], writes=[tan])
                cur, curn, oth, othn, wd, step = ta_, tan, tb_, tbn, XW - 1, 2
                while step * 2 < w:
                    nwd = wd - step
                    P.op(eng, TT(oth[:, 0:nwd], cur[:, 0:nwd], cur[:, step:step + nwd], ALU.add), reads=[curn], writes=[othn])
                    cur, curn, oth, othn, wd, step = oth, othn, cur, curn, nwd, step * 2
                P.op(eng, TT(acc[:], cur[:, j0:j0 + BLK], cur[:, j0 + step:j0 + step + BLK], ALU.add), reads=[curn], writes=[accn])
            P.op(eng, TT(acc[:], acc[:], ic[:, gi * BLK:(gi + 1) * BLK], ALU.mult), reads=[accn, icn], writes=[accn])
            P.op(eng, TT(PT[:, kc, :], acc[:], XM(kc), ALU.subtract), reads=[accn, XR[kc]], writes=['b_HT0'])
        pw = d['poolwb'].rearrange("g (k p) n -> g p k n", p=128)
        for gi in range(4):
            Wt, wn = wload(lambda buf, nm, gi=gi: P.dma('sp', DMA(buf[:, 0:4, :], pw[gi]), reads=['wscr'], writes=[nm]))
            for dq in range(4):
                kc_o = gi * 4 + dq
                ps, bn = mm16(lambda k, Wt=Wt, dq=dq: Wt[:, k, dq * 128:(dq + 1) * 128], lambda k, gi=gi: PT[:, gi * 4 + k, :], [wn, 'b_HT0'], nk=4)
                sg = st[4 + (dq % 2)]
                sgn = 'b_st%d' % (4 + (dq % 2))
                P.op('act', ACTF(sg[:], ps[:, :], AF.Copy, scale=c['pscale'][:, kc_o:kc_o + 1]), reads=[bn, 'k_pscale'], writes=[sgn])
                P.op('dve', STT(XM(kc_o), XM(kc_o), ALPHA, sg[:], ALU.mult, ALU.add), reads=[sgn, XR[kc_o]], writes=[XR[kc_o]])

    def mixer_attn(self, s, t0, X, XM, Xb, HT, W, st, pt, wload, mm16, XR, XBR):
        P, d, c = self.P, self.d, self.c
        QT = HT
        OT = HT
        cs, sn = HT[:, 40:42, :].rearrange("p a b -> p (a b)").bitcast(F32), HT[:, 42:44, :].rearrange("p a b -> p (a b)").bitcast(F32)
        P.dma('sp', DMA(cs, d['c_cos'][:, t0:t0 + BLK]), writes=['a_cos'])
        P.dma('sp', DMA(sn, d['c_sin'][:, t0:t0 + BLK]), writes=['a_sin'])
        for kc in range(NKC):
            P.op('pool' if kc % 2 else 'act', CP(Xb[:, kc, :], XM(kc)) if kc % 2 else ACTF(Xb[:, kc, :], XM(kc), AF.Copy),
                 reads=[XR[kc]], writes=[XBR[kc]])
        wv = d['wqkvb'].rearrange("(k p) n -> p k n", p=128)
        Tsets = [[(st[i_], 'b_st%d' % i_) for i_ in range(5)], [(self.st2[i_], 'b_sx%d' % i_) for i_ in range(5)]]

        class _N:
            pass
        qps = {}
        wcur = {}

        def emit_q(h):
            hg, hq = divmod(h, 4)
            if hq == 0:
                wcur['W'] = wload(lambda buf, nm, hg=hg: P.dma('sp', DMA(buf[:], wv[:, :, hg * 512:(hg + 1) * 512]), reads=['wscr'], writes=[nm]))
            Wt, wn = wcur['W']
            qps[h] = mm16(lambda kc, Wt=Wt, hq=hq: Wt[:, kc, hq * 128:(hq + 1) * 128], lambda kc: Xb[:, kc, :], [wn] + XBR)
        emit_q(0)
        emit_q(1)
        for h in range(16):
            if h + 2 < 16:
                emit_q(h + 2)
            ps, bn = qps.pop(h)
            self._qk(ps, bn, QT[:, h, :], 'b_QT%d' % h, 0, Tsets[h % 2], cs, sn)
        scale = 128 ** -0.5
        for kvh in range(4):
            u = self._wrr
            self._wrr ^= 1
            KV = W[u]
            kvn = 'b_W%d' % u
            KT = KV[:, 0:8, :].rearrange("p a b -> p (a b)")
            Vv = KV[:, 8:16, :].rearrange("p a b -> p (a b)").rearrange("p (sc e) -> p sc e", e=128)
            P.dma('sp', DMA(KT, d['kts'][s, kvh]), reads=['kts'], writes=[kvn])
            P.dma('sp', DMA(Vv, d['vs'][s, kvh]), reads=['vs'], writes=[kvn])
            for hq in range(4):
                h = kvh * 4 + hq
                bo = self.next_bank()
                bd = self.next_bank()
                pso, psd = self.banks[bo], self.banks[bd]
                dacc, daccn = self.dacc[h % 2], 'b_dacc%d' % (h % 2)
                sbank = {}

                def emitS(sc, h=h, bo=bo, bd=bd, KT=KT, kvn=kvn):
                    bs = self.next_bank()
                    while bs in (bo, bd):
                        bs = self.next_bank()
                    P.op('pe', MM(self.banks[bs][:, :], KT[:, sc * 128:(sc + 1) * 128], QT[:, h, :]), reads=[kvn, 'b_QT%d' % h], writes=['bank%d' % bs])
                    sbank[sc] = bs
                emitS(0)
                emitS(1)
                emitS(2)
                for sc in range(32):
                    bs = sbank[sc]
                    pss = self.banks[bs]
                    pu = self._ptrr
                    self._ptrr = (self._ptrr + 1) % 8
                    P.op('act', ACTF(pt[pu][:], pss[:, :], AF.Exp, scale=scale), reads=['bank%d' % bs], writes=['b_pt%d' % pu])
                    if sc + 3 < 32:
                        emitS(sc + 3)
                    if sc % 2 == 0:
                        P.op('pe', [MM(pso[:, :], Vv[:, sc, :], pt[pu][:], sc == 0, sc == 31), MM(psd[:, :], self.onesbs[:], pt[pu][:], sc == 0, False)],
                             reads=[kvn, 'b_pt%d' % pu, 'b_onesbs'], writes=['bank%d' % bo, 'bank%d' % bd])
                    else:
                        P.op('pe', MM(pso[:, :], Vv[:, sc, :], pt[pu][:], sc == 0, sc == 31), reads=[kvn, 'b_pt%d' % pu], writes=['bank%d' % bo])
                        if sc == 1:
                            P.op('dve', CP(dacc[:], pt[pu][:]), reads=['b_pt%d' % pu], writes=[daccn])
                        else:
                            P.op('dve', TT(dacc[:], dacc[:], pt[pu][:], ALU.add), reads=['b_pt%d' % pu, daccn], writes=[daccn])
                P.op('pe', MM(psd[:, :], c['ones_rms'][:], dacc[:], False, True), reads=[daccn, 'k_ones_rms'], writes=['bank%d' % bd])
                rd = st[5]
                P.op('dve', lambda e, o=rd[:], a=psd[:, :]: e.reciprocal(out=o, in_=a), reads=['bank%d' % bd], writes=['b_st5'])
                P.op('dve', STT(OT[:, 16 + h, :], pso[:, :], 1.0 / 128, rd[:], ALU.mult, ALU.mult), reads=['bank%d' % bo, 'b_st5'], writes=['b_OT'])
        wo = d['wob'].rearrange("(k p) n -> p k n", p=128)
        for dg in range(4):
            Wt, wn = wload(lambda buf, nm, dg=dg: P.dma('sp', DMA(buf[:], wo[:, :, dg * 512:(dg + 1) * 512]), reads=['wscr'], writes=[nm]))
            for dq in range(4):
                kc_o = dg * 4 + dq
                ps, bn = mm16(lambda kc, Wt=Wt, dq=dq: Wt[:, kc, dq * 128:(dq + 1) * 128], lambda kc: OT[:, 16 + kc, :], [wn, 'b_OT'])
                P.op('dve', STT(XM(kc_o), XM(kc_o), ALPHA, ps[:, :], ALU.mult, ALU.add), reads=[bn, XR[kc_o]], writes=[XR[kc_o]])

    def _qk(self, ps, bn, out_bf, out_name, which, T, cs, sn):
        P, c = self.P, self.c
        (kq, nkq), (ksq, nksq), (rs, nrs), (kh, nkh), (ta, nta) = T
        P.op('act', ACTF(kq[:], ps[:, :], AF.Copy), reads=[bn], writes=[nkq])
        P.op('pool', TT(ksq[:], kq[:], kq[:], ALU.mult), reads=[nkq], writes=[nksq])
        b2 = self.next_bank()
        ps2 = self.banks[b2]
        P.op('pe', MM(ps2[:, :], c['ones_rms'][:], ksq[:]), reads=[nksq, 'k_ones_rms'], writes=['bank%d' % b2])
        P.op('act', ACTF(rs[:], ps2[:, :], AF.Sqrt, bias=c['eps'][:, 1:2]), reads=['bank%d' % b2, 'k_eps'], writes=[nrs])
        P.op('dve', lambda e, o=rs[:]: e.reciprocal(out=o, in_=o), reads=[nrs], writes=[nrs])
        P.op('dve', STT(kh[:], kq[:], c['qkn'][:, which:which + 1], rs[:], ALU.mult, ALU.mult), reads=[nkq, nrs, 'k_qkn'], writes=[nkh])
        b3 = self.next_bank()
        ps3 = self.banks[b3]
        P.op('pe', MM(ps3[:, :], c['rot'][:], kh[:]), reads=[nkh, 'k_rot'], writes=['bank%d' % b3])
        P.op('pool', TT(ta[:], kh[:], cs, ALU.mult), reads=[nkh, 'a_cos'], writes=[nta])
        P.op('dve', TT(rs[:], ps3[:, :], sn, ALU.mult), reads=['bank%d' % b3, 'a_sin'], writes=[nrs])
        P.op('pool', TT(out_bf, ta[:], rs[:], ALU.add), reads=[nta, nrs], writes=[out_name])


def make_consts():
    bf = ml_dtypes.bfloat16
    k = {}
    k['c_ident'] = np.eye(128, dtype=np.float32)
    k['c_identb'] = np.eye(128, dtype=np.float32).astype(bf)
    sw = np.zeros((128, 128), np.float32)
    for i in range(128):
        sw[i, (i + 64) % 128] = 1.0
    k['c_swap'] = sw
    R = np.zeros((128, 128), np.float32)
    for m in range(128):
        if m % 64 < 32:
            R[m, m + 32] = -1.0
        else:
            R[m, m - 32] = 1.0
    k['c_rot'] = np.ascontiguousarray(R.T)
    k['c_ones_ln'] = np.full((128, 128), 1.0 / D, np.float32)
    k['c_ones_rms'] = np.full((128, 128), 1.0 / 128, np.float32)
    k['c_onesb'] = np.ones((128, 128), np.float32).astype(bf)
    tau = np.arange(128) // 16
    k['c_maskf'] = (tau[None, :] >= tau[:, None]).astype(np.float32)
    k['c_maskb'] = (tau[None, :] <= tau[:, None]).astype(np.float32)
    sel = np.zeros((16, 128), np.float32)
    for q in range(16):
        sel[q, q::16] = 1.0
    k['c_sel'] = sel
    t = np.arange(L)
    row = (t // 64).astype(np.float32)
    col = (t % 64).astype(np.float32)
    inv_freq = np.power(np.float32(10000.0), -np.arange(32, dtype=np.float32) / np.float32(32)).astype(np.float32)
    cos = np.zeros((128, L), np.float32)
    sin = np.zeros((128, L), np.float32)
    for hd in range(128):
        f = hd % 32
        pos = row if hd < 64 else col
        ang = (pos * inv_freq[f]).astype(np.float32)
        cos[hd] = np.cos(ang)
        sin[hd] = np.sin(ang)
    k['c_cos'] = cos
    k['c_sin'] = sin
    invc = np.zeros((4, L), np.float32)
    for gi, w in enumerate((2, 4, 8, 16)):
        lo = np.clip(t - w // 2, 0, L)
        hi = np.clip(t + w // 2, 0, L)
        invc[gi] = 1.0 / (hi - lo).astype(np.float32)
    k['c_invc'] = np.ascontiguousarray(np.broadcast_to(invc[None], (128, 4, L))).astype(np.float32)
    return k


PARAM_NAMES = ['s5_a_re', 's5_a_im', 's5_log_step', 's5_b_re', 's5_b_im', 's5_c_re', 's5_c_im', 's5_d', 's5_w_out',
               's5_w_gate', 'pool_w', 'pool_scale', 'attn_w_qkv', 'attn_q_norm', 'attn_k_norm', 'attn_w_o',
               'ln1_g', 'ln1_b', 'ln2_g', 'ln2_b', 'mlp_w1', 'mlp_w2']


def kernel(x_prompt, x_sample, **params):
    x_prompt = np.asarray(x_prompt, np.float32)
    x_sample = np.asarray(x_sample, np.float32)
    nc = Builder(nseq=2).build()
    consts = make_consts()
    base = {n: np.ascontiguousarray(np.asarray(params[n], np.float32)) for n in PARAM_NAMES}
    base.update(consts)
    in_maps = []
    for i in range(8):
        m = dict(base)
        m['x_in'] = np.ascontiguousarray(np.stack([x_sample[i], x_prompt[i % 2]], axis=0))
        in_maps.append(m)
    res = run_bass_kernel_spmd(nc, in_maps, core_ids=list(range(8)))
    y_sample = np.stack([np.asarray(res.results[i]['y_out'][0]) for i in range(8)], axis=0).astype(np.float32)
    y_prompt = np.stack([np.asarray(res.results[i]['y_out'][1]) for i in range(2)], axis=0).astype(np.float32)
    return (y_prompt, y_sample)
```

```python
import math
from contextlib import ExitStack
import numpy as np
import ml_dtypes
import concourse.bass as bass
import concourse.mybir as mybir
from concourse.bass_utils import run_bass_kernel_spmd

F32 = mybir.dt.float32
BF16 = mybir.dt.bfloat16
AF = mybir.ActivationFunctionType
ALU = mybir.AluOpType

D = 2048
L = 4096
DFF = 8192
NKC = 16
BLK = 512
NBLK = L // BLK
ALPHA = (2 * 4) ** 0.25
LN_EPS = 1e-5
RMS_EPS = 1e-6
PAD = 8
XW = BLK + 2 * PAD
NSTEP = 9
GC = 1.5957691216057308


class Prog:
    NDMA = 24

    def __init__(self, nc, es):
        self.nc = nc
        self.engs = dict(pe=nc.tensor, act=nc.scalar, dve=nc.vector, pool=nc.gpsimd, sp=nc.sync)
        self.sems = {}
        for e in ['pe', 'act', 'dve', 'pool']:
            self.sems[('c', e)] = es.enter_context(nc.semaphore("c_" + e))
        for i in range(self.NDMA):
            self.sems[('d', i)] = es.enter_context(nc.semaphore("d%d" % i))
        self.tot = {k: 0 for k in self.sems}
        self.q = {e: [] for e in self.engs}
        self.waited = {e: {} for e in self.engs}
        self.res = {}
        self.rr = 0
        self.rr2 = 0
        self.nops = 0

    def _wait(self, eng, key, val):
        if val <= 0:
            return
        if self.waited[eng].get(key, 0) < val:
            self.q[eng].append(('w', key, val))
            self.waited[eng][key] = val

    def _deps(self, eng, reads, writes):
        for r in reads:
            st = self.res.get(r)
            if st and st[0]:
                self._wait(eng, *st[0])
        for w in writes:
            st = self.res.get(w)
            if st:
                if st[0]:
                    self._wait(eng, *st[0])
                for k, v in st[1].items():
                    self._wait(eng, k, v)

    def _update(self, tok, reads, writes):
        for r in reads:
            st = self.res.setdefault(r, [None, {}])
            if st[1].get(tok[0], 0) < tok[1]:
                st[1][tok[0]] = tok[1]
        for w in writes:
            self.res[w] = [tok, {}]

    def op(self, eng, fns, reads=(), writes=()):
        if not isinstance(fns, (list, tuple)):
            fns = [fns]
        self._deps(eng, reads, writes)
        key = ('c', eng)
        self.tot[key] += 1
        tok = (key, self.tot[key])
        self.q[eng].append(('i', list(fns), key, 1))
        self._update(tok, reads, writes)
        self.nops += len(fns)

    def dma(self, queue, fn, reads=(), writes=(), grp=0):
        if grp == 0:
            i = self.rr
            self.rr = (self.rr + 1) % (self.NDMA - 8)
        else:
            i = self.NDMA - 8 + self.rr2
            self.rr2 = (self.rr2 + 1) % 8
        key = ('d', i)
        if grp == 0:
            self._wait(queue, key, self.tot[key])
        self._deps(queue, reads, writes)
        self.tot[key] += 16
        tok = (key, self.tot[key])
        self.q[queue].append(('i', [fn], key, 16))
        self._update(tok, reads, writes)
        self.nops += 1

    def barrier(self):
        for e in self.engs:
            for k, v in self.tot.items():
                self._wait(e, k, v)
        self.res = {}

    def flush(self):
        nc = self.nc
        with nc.Block() as blk:
            def runner(items, sems):
                def body(eng):
                    for it in items:
                        if it[0] == 'w':
                            eng.wait_ge(sems[it[1]], it[2])
                        else:
                            last = None
                            for f in it[1]:
                                last = f(eng)
                            last.then_inc(sems[it[2]], it[3])
                return body
            deco = dict(pe=blk.tensor, act=blk.scalar, dve=blk.vector, pool=blk.gpsimd, sp=blk.sync)
            for e in self.engs:
                if self.q[e]:
                    deco[e](runner(self.q[e], self.sems))
        self.q = {e: [] for e in self.engs}


def TT(out, a, b, op):
    return lambda e: e.tensor_tensor(out=out, in0=a, in1=b, op=op)


def TS(out, a, s1, op0, s2=None, op1=None):
    if op1 is None:
        return lambda e: e.tensor_scalar(out=out, in0=a, scalar1=s1, scalar2=None, op0=op0)
    return lambda e: e.tensor_scalar(out=out, in0=a, scalar1=s1, scalar2=s2, op0=op0, op1=op1)


def STT(out, a, s, b, op0, op1):
    return lambda e: e.scalar_tensor_tensor(out=out, in0=a, scalar=s, in1=b, op0=op0, op1=op1)


def CP(out, a):
    return lambda e: e.tensor_copy(out=out, in_=a)


def ACTF(out, a, func, scale=1.0, bias=0.0):
    return lambda e: e.activation(out=out, in_=a, func=func, bias=bias, scale=scale)


def MM(out, lhsT, rhs, start=True, stop=True):
    return lambda e: e.matmul(out, lhsT, rhs, start=start, stop=stop)


def TR(out, a, ident):
    return lambda e: e.transpose(out, a, ident)


def DMA(out, a, slow=False):
    if slow:
        return lambda e: e.dma_start(out=out, in_=a, allow_slow_non_contiguous=True)
    return lambda e: e.dma_start(out=out, in_=a)


def MEMSET(out, v):
    return lambda e: e.memset(out, v)


class Builder:
    def __init__(self, nseq=2, layers=(0, 1, 2, 3), nblk=NBLK, do_prep=True, debug_y=False):
        self.nseq = nseq
        self.layers = tuple(layers)
        self.nblk = nblk
        self.do_prep = do_prep
        self.debug_y = debug_y

    def declare(self):
        nc = self.nc
        ns = self.nseq

        def inp(name, shape, dt=F32):
            return nc.dram_tensor(name, list(shape), dt, kind="ExternalInput").ap()

        def scr(name, shape, dt=F32):
            return nc.dram_tensor(name, list(shape), dt, kind="Internal").ap()
        d = {}
        d['x_in'] = inp('x_in', [ns, L, D])
        d['y_out'] = nc.dram_tensor('y_out', [ns, L, D], F32, kind="ExternalOutput").ap()
        for nm, shp in [('s5_a_re', [2, 2, 128, 64]), ('s5_a_im', [2, 2, 128, 64]), ('s5_log_step', [2, 2, 128]),
                        ('s5_b_re', [2, 2, 128, 64, 16]), ('s5_b_im', [2, 2, 128, 64, 16]),
                        ('s5_c_re', [2, 2, 128, 16, 64]), ('s5_c_im', [2, 2, 128, 16, 64]),
                        ('s5_d', [2, D]), ('s5_w_out', [2, D, D]), ('s5_w_gate', [2, D, D]),
                        ('pool_w', [1, 4, 512, 512]), ('pool_scale', [1, D]),
                        ('attn_w_qkv', [1, D, 3072]), ('attn_q_norm', [1, 128]), ('attn_k_norm', [1, 128]),
                        ('attn_w_o', [1, D, D]), ('ln1_g', [4, D]), ('ln1_b', [4, D]), ('ln2_g', [4, D]),
                        ('ln2_b', [4, D]), ('mlp_w1', [4, D, DFF]), ('mlp_w2', [4, DFF, D])]:
            d[nm] = inp(nm, shp)
        d['c_ident'] = inp('c_ident', [128, 128])
        d['c_identb'] = inp('c_identb', [128, 128], BF16)
        d['c_swap'] = inp('c_swap', [128, 128])
        d['c_rot'] = inp('c_rot', [128, 128])
        d['c_ones_ln'] = inp('c_ones_ln', [128, 128])
        d['c_ones_rms'] = inp('c_ones_rms', [128, 128])
        d['c_onesb'] = inp('c_onesb', [128, 128], BF16)
        d['c_maskf'] = inp('c_maskf', [128, 128])
        d['c_maskb'] = inp('c_maskb', [128, 128])
        d['c_sel'] = inp('c_sel', [16, 128])
        d['c_cos'] = inp('c_cos', [128, L])
        d['c_sin'] = inp('c_sin', [128, L])
        d['c_invc'] = inp('c_invc', [128, 4, L])
        d['actA'] = scr('actA', [ns, L, D])
        d['actB'] = scr('actB', [ns, L, D])
        d['ys5'] = scr('ys5', [ns, L, D])
        d['w1b'] = scr('w1b', [4, D, DFF], BF16)
        d['w2b'] = scr('w2b', [4, DFF, D], BF16)
        d['woutb'] = scr('woutb', [2, D, D], BF16)
        d['wgateb'] = scr('wgateb', [2, D, D], BF16)
        d['poolwb'] = scr('poolwb', [4, 512, 512], BF16)
        d['wqkvb'] = scr('wqkvb', [D, 3072], BF16)
        d['wob'] = scr('wob', [D, D], BF16)
        d['kts'] = scr('kts', [ns, 4, 128, L], BF16)
        d['vs'] = scr('vs', [ns, 4, 128, 32, 128], BF16)
        d['rt32'] = scr('rt32', [2, 2, 128, 128 * 128])
        d['ry32'] = scr('ry32', [2, 2, 128, 128 * 128])
        d['ot32'] = scr('ot32', [2, 2, 128, 128 * 128])
        d['rt16'] = scr('rt16', [2, 2, 128, 128 * 128], BF16)
        d['ot16'] = scr('ot16', [2, 2, 128, 128 * 128], BF16)
        d['mt16'] = scr('mt16', [2, 128, 128, 128], BF16)
        d['pm16'] = scr('pm16', [2, 2, 128, 128, NSTEP * 128], BF16)
        self.d = d

    def sb(self, es, name, shape, dt=F32):
        self._uid = getattr(self, '_uid', 0) + 1
        return es.enter_context(self.nc.sbuf_tensor("%s_u%d" % (name, self._uid), list(shape), dt))

    def next_bank(self):
        b = self.bank_rr
        self.bank_rr = (self.bank_rr + 1) % len(self.banks)
        return b

    def build(self):
        nc = bass.Bass("TRN2", target_bir_lowering=False)
        self.nc = nc
        self.declare()
        with ExitStack() as es:
            P = Prog(nc, es)
            self.P = P
            self.banks = [es.enter_context(nc.psum_tensor("ps%d" % i, [128, 512], F32)) for i in range(7)]
            self.bank16 = es.enter_context(nc.psum_tensor("ps16", [128, 1024], BF16))
            self.bank_rr = 0
            c = {}
            c['ident'] = self.sb(es, 'k_ident', [128, 128])
            c['identb'] = self.sb(es, 'k_identb', [128, 128], BF16)
            c['ones_ln'] = self.sb(es, 'k_ones_ln', [128, 128])
            c['ones_rms'] = self.sb(es, 'k_ones_rms', [128, 128])
            c['onesb'] = self.sb(es, 'k_onesb', [128, 128], BF16)
            c['rot'] = self.sb(es, 'k_rot', [128, 128])
            c['lnp'] = self.sb(es, 'k_lnp', [128, 4, 4, NKC])
            c['pscale'] = self.sb(es, 'k_pscale', [128, NKC])
            c['qkn'] = self.sb(es, 'k_qkn', [128, 2])
            c['eps'] = self.sb(es, 'k_eps', [128, 2])
            self.c = c
            d = self.d
            P.op('dve', [MEMSET(c['eps'][:, 0:1], LN_EPS), MEMSET(c['eps'][:, 1:2], RMS_EPS)], writes=['k_eps'])
            for nm in ['ident', 'identb', 'ones_ln', 'ones_rms', 'onesb', 'rot']:
                P.dma('sp', DMA(c[nm][:], d['c_' + nm]), writes=['k_' + nm])
            for wi, nm in enumerate(['ln1_g', 'ln1_b', 'ln2_g', 'ln2_b']):
                for l in range(4):
                    P.dma('sp', DMA(c['lnp'][:, wi, l, :], d[nm][l].rearrange("(k p) -> p k", p=128), slow=True),
                          writes=['k_lnp'])
            P.dma('sp', DMA(c['pscale'][:], d['pool_scale'][0].rearrange("(k p) -> p k", p=128), slow=True),
                  writes=['k_pscale'])
            P.dma('sp', DMA(c['qkn'][:, 0:1], d['attn_q_norm'][0].rearrange("(p o) -> p o", o=1), slow=True),
                  writes=['k_qkn'])
            P.dma('sp', DMA(c['qkn'][:, 1:2], d['attn_k_norm'][0].rearrange("(p o) -> p o", o=1), slow=True),
                  writes=['k_qkn'])
            if self.do_prep:
                self.convert_weights()
                if 0 in self.layers or 3 in self.layers:
                    self.s5_prep()
                else:
                    P.barrier()
                    P.flush()
            cur = d['x_in']
            pp = [d['actA'], d['actB']]
            for li, l in enumerate(self.layers):
                src = cur
                dst = d['y_out'] if li == len(self.layers) - 1 else pp[li % 2]
                cur = dst
                if l % 3 == 0:
                    self.s5_scan(l // 3, src)
                if l % 3 == 2:
                    self.kv_phase(src)
                if not (self.debug_y and l % 3 == 0):
                    self.block_phase(l, src, dst)
                else:
                    self.copy_debug(d['ys5'], dst)
            P.barrier()
            P.flush()
        return nc

    def convert_weights(self):
        P, d = self.P, self.d

        def conv(dst, src, rows, rstep):
            for r0 in range(0, rows, rstep):
                P.dma('pool', DMA(dst[r0:r0 + rstep], src[r0:r0 + rstep]), writes=['wscr'], grp=1)
        need_mlp = sorted(set(self.layers))
        for l in need_mlp:
            conv(d['w1b'][l], d['mlp_w1'][l], D, 256)
            conv(d['w2b'][l], d['mlp_w2'][l], DFF, 1024)
            if l % 3 == 0:
                j = l // 3
                conv(d['woutb'][j], d['s5_w_out'][j], D, 1024)
                conv(d['wgateb'][j], d['s5_w_gate'][j], D, 1024)
            if l % 3 == 1:
                conv(d['poolwb'].rearrange("g a b -> (g a) b"), d['pool_w'][0].rearrange("g a b -> (g a) b"), 2048, 2048)
            if l % 3 == 2:
                conv(d['wqkvb'], d['attn_w_qkv'][0], D, 512)
                conv(d['wob'], d['attn_w_o'][0], D, 1024)

    def copy_debug(self, src, dst):
        P = self.P
        for s in range(self.nseq):
            for r0 in range(0, L, 512):
                P.dma('sp', DMA(dst[s, r0:r0 + 512], src[s, r0:r0 + 512]), reads=['ys5'], writes=['dst'])

    def s5_prep(self):
        nc, P, d, c = self.nc, self.P, self.d, self.c
        js = sorted(set(l // 3 for l in self.layers if l % 3 == 0))
        with ExitStack() as es:
            sb = lambda n, s, dt=F32: self.sb(es, n, s, dt)
            are, aim, ls = sb('p_are', [128, 64]), sb('p_aim', [128, 64]), sb('p_ls', [128, 1])
            bre, bim = sb('p_bre', [128, 64, 16]), sb('p_bim', [128, 64, 16])
            cre, cim = sb('p_cre', [128, 16, 64]), sb('p_cim', [128, 16, 64])
            bbr, bbi = sb('p_bbr', [128, 64, 16]), sb('p_bbi', [128, 64, 16])
            pwr, pwi = sb('p_pwr', [128, 17, 64]), sb('p_pwi', [128, 17, 64])
            dpr, dpi = sb('p_dpr', [128, NSTEP, 64]), sb('p_dpi', [128, NSTEP, 64])
            tmp = [sb('p_t%d' % i, [128, 1024]) for i in range(4)]
            sm = [sb('p_s%d' % i, [128, 64]) for i in range(8)]
            big = [sb('p_big%d' % i, [128, 16384]) for i in range(1)]
            ss = sb('p_ss', [128, 2, 128])
            s1i = sb('p_qi', [128, 64], mybir.dt.int32)
            T1 = sb('p_T1', [128, 4, NSTEP, 128], BF16)
            T2 = sb('p_T2', [128, 4, NSTEP, 128], BF16)
            bigrr = [0]

            def V(eng, fn, reads, writes):
                P.op(eng, fn, reads=reads, writes=writes)

            def cmul(outr, outi, ar, ai, br, bi, t0, t1, names_in, names_out, eng='dve'):
                V(eng, TT(t0, ar, br, ALU.mult), names_in, ['p_t0'])
                V(eng, TT(t1, ai, bi, ALU.mult), names_in, ['p_t1'])
                V(eng, TT(outr, t0, t1, ALU.subtract), ['p_t0', 'p_t1'], names_out)
                V(eng, TT(t0, ar, bi, ALU.mult), names_in, ['p_t0'])
                V(eng, TT(t1, ai, br, ALU.mult), names_in, ['p_t1'])
                V(eng, TT(outi, t0, t1, ALU.add), ['p_t0', 'p_t1'], names_out)

            for j in js:
                for dr in range(2):
                    jd = j * 2 + dr
                    ld = lambda out, src, nm: P.dma('sp', DMA(out, src), writes=[nm])
                    ld(are[:], d['s5_a_re'][j, dr], 'p_are')
                    ld(aim[:], d['s5_a_im'][j, dr], 'p_aim')
                    P.dma('sp', DMA(ls[:], d['s5_log_step'][j, dr].rearrange("(p o) -> p o", o=1), slow=True), writes=['p_ls'])
                    ld(bre[:], d['s5_b_re'][j, dr], 'p_bre')
                    ld(bim[:], d['s5_b_im'][j, dr], 'p_bim')
                    ld(cre[:], d['s5_c_re'][j, dr], 'p_cre')
                    ld(cim[:], d['s5_c_im'][j, dr], 'p_cim')
                    dt_, zr, zi, mag, magi, cs, sn, tq = sm
                    s0 = tmp[0][:, 0:64]
                    s1 = tmp[1][:, 0:64]
                    V('act', ACTF(dt_[:, 0:1], ls[:], AF.Exp), ['p_ls'], ['p_dt'])
                    V('dve', TS(are[:], are[:], -1e-4, ALU.min), ['p_are'], ['p_are'])
                    V('dve', TS(zr[:], are[:], dt_[:, 0:1], ALU.mult), ['p_are', 'p_dt'], ['p_zr'])
                    V('dve', TS(zi[:], aim[:], dt_[:, 0:1], ALU.mult), ['p_aim', 'p_dt'], ['p_zi'])
                    V('act', ACTF(mag[:], zr[:], AF.Exp), ['p_zr'], ['p_mag'])
                    V('act', ACTF(magi[:], zr[:], AF.Exp, scale=-1.0), ['p_zr'], ['p_magi'])
                    def rred(dst, shift, tname):
                        V('dve', TS(dst, zi[:], shift, ALU.add), ['p_zi'], [tname])
                        V('dve', TS(s1i[:], dst, 1.0 / (2 * math.pi), ALU.mult), [tname], ['p_qi'])
                        V('dve', CP(tq[:], s1i[:]), ['p_qi'], ['p_tq'])
                        V('dve', STT(dst, tq[:], -2 * math.pi, dst, ALU.mult, ALU.add), ['p_tq', tname], [tname])
                        V('dve', TS(tq[:], dst, math.pi, ALU.is_gt), [tname], ['p_tq'])
                        V('dve', STT(dst, tq[:], -2 * math.pi, dst, ALU.mult, ALU.add), ['p_tq', tname], [tname])
                        V('dve', TS(tq[:], dst, -math.pi, ALU.is_lt), [tname], ['p_tq'])
                        V('dve', STT(dst, tq[:], 2 * math.pi, dst, ALU.mult, ALU.add), ['p_tq', tname], [tname])
                    rred(s0, 0.0, 'p_t0')
                    V('act', ACTF(sn[:], s0, AF.Sin), ['p_t0'], ['p_sn'])
                    rred(s1, 0.5 * math.pi, 'p_t1')
                    V('act', ACTF(cs[:], s1, AF.Sin), ['p_t1'], ['p_cs'])
                    K0 = 8
                    V('dve', MEMSET(pwr[:, K0, :], 1.0), [], ['p_pw'])
                    V('dve', MEMSET(pwi[:, K0, :], 0.0), [], ['p_pw'])
                    V('dve', TT(pwr[:, K0 + 1, :], mag[:], cs[:], ALU.mult), ['p_mag', 'p_cs'], ['p_pw'])
                    V('dve', TT(pwi[:, K0 + 1, :], mag[:], sn[:], ALU.mult), ['p_mag', 'p_sn'], ['p_pw'])
                    V('dve', TT(pwr[:, K0 - 1, :], magi[:], cs[:], ALU.mult), ['p_magi', 'p_cs'], ['p_pw'])
                    V('dve', STT(pwi[:, K0 - 1, :], magi[:], -1.0, sn[:], ALU.mult, ALU.mult), ['p_magi', 'p_sn'], ['p_pw'])
                    for k in range(2, 9):
                        cmul(pwr[:, K0 + k, :], pwi[:, K0 + k, :], pwr[:, K0 + k - 1, :], pwi[:, K0 + k - 1, :],
                             pwr[:, K0 + 1, :], pwi[:, K0 + 1, :], s0, s1, ['p_pw'], ['p_pw'])
                        cmul(pwr[:, K0 - k, :], pwi[:, K0 - k, :], pwr[:, K0 - k + 1, :], pwi[:, K0 - k + 1, :],
                             pwr[:, K0 - 1, :], pwi[:, K0 - 1, :], s0, s1, ['p_pw'], ['p_pw'])
                    V('dve', CP(dpr[:, 0, :], pwr[:, K0 + 8, :]), ['p_pw'], ['p_dp'])
                    V('dve', CP(dpi[:, 0, :], pwi[:, K0 + 8, :]), ['p_pw'], ['p_dp'])
                    for i in range(1, NSTEP):
                        cmul(dpr[:, i, :], dpi[:, i, :], dpr[:, i - 1, :], dpi[:, i - 1, :], dpr[:, i - 1, :], dpi[:, i - 1, :],
                             s0, s1, ['p_dp'], ['p_dp'])
                    lr, li = pwr[:, K0 + 1, :], pwi[:, K0 + 1, :]
                    cr_, ci_ = zr, zi
                    V('dve', TS(tq[:], lr, -1.0, ALU.add), ['p_pw'], ['p_tq'])
                    V('dve', TT(s0, are[:], are[:], ALU.mult), ['p_are'], ['p_t0'])
                    V('dve', TT(s1, aim[:], aim[:], ALU.mult), ['p_aim'], ['p_t1'])
                    V('dve', TT(s0, s0, s1, ALU.add), ['p_t0', 'p_t1'], ['p_t0'])
                    V('dve', lambda e, o=mag[:], a=s0: e.reciprocal(out=o, in_=a), ['p_t0'], ['p_mag'])
                    V('dve', TT(s0, tq[:], are[:], ALU.mult), ['p_tq', 'p_are'], ['p_t0'])
                    V('dve', TT(s1, li, aim[:], ALU.mult), ['p_pw', 'p_aim'], ['p_t1'])
                    V('dve', TT(s0, s0, s1, ALU.add), ['p_t0', 'p_t1'], ['p_t0'])
                    V('dve', TT(cr_[:], s0, mag[:], ALU.mult), ['p_t0', 'p_mag'], ['p_zr'])
                    V('dve', TT(s0, li, are[:], ALU.mult), ['p_pw', 'p_are'], ['p_t0'])
                    V('dve', TT(s1, tq[:], aim[:], ALU.mult), ['p_tq', 'p_aim'], ['p_t1'])
                    V('dve', TT(s0, s0, s1, ALU.subtract), ['p_t0', 'p_t1'], ['p_t0'])
                    V('dve', TT(ci_[:], s0, mag[:], ALU.mult), ['p_t0', 'p_mag'], ['p_zi'])
                    crb = cr_[:].unsqueeze(2).broadcast_to([128, 64, 16])
                    cib = ci_[:].unsqueeze(2).broadcast_to([128, 64, 16])
                    t0 = tmp[0][:].rearrange("p (n q) -> p n q", q=16)
                    t1 = tmp[1][:].rearrange("p (n q) -> p n q", q=16)
                    cmul(bbr[:], bbi[:], crb, cib, bre[:], bim[:], t0, t1, ['p_zr', 'p_zi', 'p_bre', 'p_bim'], ['p_bb'])

                    def getbig():
                        return big[0], 'p_big0'

                    def pw(k):
                        return (pwr[:, K0 + k, :], pwi[:, K0 + k, :])
                    engs = ['dve', 'pool']

                    bg, bn = getbig()
                    bv = bg[:].rearrange("p (t q n) -> p t n q", t=8, q=16, n=128)
                    for tau in range(8):
                        k = (7 - tau) if dr == 0 else tau
                        pr, pi = pw(k)
                        prb = pr.unsqueeze(2).broadcast_to([128, 64, 16])
                        pib = pi.unsqueeze(2).broadcast_to([128, 64, 16])
                        eng = engs[tau % 2]
                        ta = tmp[(tau % 2) * 2][:].rearrange("p (n q) -> p n q", q=16)
                        tb = tmp[(tau % 2) * 2 + 1][:].rearrange("p (n q) -> p n q", q=16)
                        na, nb = 'p_t%d' % ((tau % 2) * 2), 'p_t%d' % ((tau % 2) * 2 + 1)
                        V(eng, TT(ta, bbr[:], prb, ALU.mult), ['p_bb', 'p_pw'], [na])
                        V(eng, TT(tb, bbi[:], pib, ALU.mult), ['p_bb', 'p_pw'], [nb])
                        V(eng, TT(bv[:, tau, 0:64, :], ta, tb, ALU.subtract), [na, nb], [bn])
                        V(eng, TT(ta, bbr[:], pib, ALU.mult), ['p_bb', 'p_pw'], [na])
                        V(eng, TT(tb, bbi[:], prb, ALU.mult), ['p_bb', 'p_pw'], [nb])
                        V(eng, TT(bv[:, tau, 64:128, :], ta, tb, ALU.add), [na, nb], [bn])
                    P.dma('sp', DMA(d['rt32'][j, dr], bg[:]), reads=[bn], writes=['rt32'])
                    bg, bn = getbig()
                    bv = bg[:].rearrange("p (n t q) -> p t n q", t=8, q=16, n=128)
                    for tau in range(8):
                        k = -(tau + 1) if dr == 0 else tau - 8
                        pr, pi = pw(k)
                        prb = pr.unsqueeze(2).broadcast_to([128, 64, 16])
                        pib = pi.unsqueeze(2).broadcast_to([128, 64, 16])
                        eng = engs[tau % 2]
                        ta = tmp[(tau % 2) * 2][:].rearrange("p (n q) -> p n q", q=16)
                        tb = tmp[(tau % 2) * 2 + 1][:].rearrange("p (n q) -> p n q", q=16)
                        na, nb = 'p_t%d' % ((tau % 2) * 2), 'p_t%d' % ((tau % 2) * 2 + 1)
                        V(eng, TT(ta, bbr[:], prb, ALU.mult), ['p_bb', 'p_pw'], [na])
                        V(eng, TT(tb, bbi[:], pib, ALU.mult), ['p_bb', 'p_pw'], [nb])
                        V(eng, TT(bv[:, tau, 0:64, :], ta, tb, ALU.subtract), [na, nb], [bn])
                        V(eng, TT(ta, bbr[:], pib, ALU.mult), ['p_bb', 'p_pw'], [na])
                        V(eng, TT(tb, bbi[:], prb, ALU.mult), ['p_bb', 'p_pw'], [nb])
                        V(eng, TT(bv[:, tau, 64:128, :], ta, tb, ALU.add), [na, nb], [bn])
                    P.dma('sp', DMA(d['ry32'][j, dr], bg[:]), reads=[bn], writes=['ry32'])
                    bg, bn = getbig()
                    bv = bg[:].rearrange("p (n t q) -> p t q n", t=8, q=16, n=128)
                    for tau in range(8):
                        k = (tau + 1) if dr == 0 else 8 - tau
                        pr, pi = pw(k)
                        prb = pr.unsqueeze(1).broadcast_to([128, 16, 64])
                        pib = pi.unsqueeze(1).broadcast_to([128, 16, 64])
                        eng = engs[tau % 2]
                        ta = tmp[(tau % 2) * 2][:].rearrange("p (q n) -> p q n", q=16)
                        tb = tmp[(tau % 2) * 2 + 1][:].rearrange("p (q n) -> p q n", q=16)
                        na, nb = 'p_t%d' % ((tau % 2) * 2), 'p_t%d' % ((tau % 2) * 2 + 1)
                        V(eng, TT(ta, cre[:], prb, ALU.mult), ['p_cre', 'p_pw'], [na])
                        V(eng, TT(tb, cim[:], pib, ALU.mult), ['p_cim', 'p_pw'], [nb])
                        V(eng, TT(bv[:, tau, :, 0:64], ta, tb, ALU.subtract), [na, nb], [bn])
                        V(eng, TT(ta, cre[:], pib, ALU.mult), ['p_cre', 'p_pw'], [na])
                        V(eng, TT(tb, cim[:], prb, ALU.mult), ['p_cim', 'p_pw'], [nb])
                        V('dve', STT(bv[:, tau, :, 64:128], ta, -1.0, tb, ALU.mult, ALU.subtract), [na, nb], [bn])
                    P.dma('sp', DMA(d['ot32'][j, dr], bg[:]), reads=[bn], writes=['ot32'])
                    for i in range(NSTEP):
                        V('dve', CP(ss[:, 0, 0:64], dpr[:, i, :]), ['p_dp'], ['p_ss'])
                        V('dve', CP(ss[:, 0, 64:128], dpr[:, i, :]), ['p_dp'], ['p_ss'])
                        V('dve', CP(ss[:, 1, 0:64], dpi[:, i, :]), ['p_dp'], ['p_ss'])
                        V('dve', TS(ss[:, 1, 64:128], dpi[:, i, :], -1.0, ALU.mult), ['p_dp'], ['p_ss'])
                        bk = self.next_bank()
                        ps = self.banks[bk]
                        P.op('pe', [TR(ps[:, 0:128], ss[:, 0, :], c['ident'][:]), TR(ps[:, 128:256], ss[:, 1, :], c['ident'][:])],
                             reads=['p_ss', 'k_ident'], writes=['bank%d' % bk])
                        V('dve', CP(T1[:, jd, i, :], ps[:, 0:128]), ['bank%d' % bk], ['p_T'])
                        V('dve', CP(T2[:, jd, i, :], ps[:, 128:256]), ['bank%d' % bk], ['p_T'])
            for j in js:
                for dr in range(2):
                    P.dma('pool', DMA(d['rt16'][j, dr], d['rt32'][j, dr]), reads=['rt32'], writes=['rt16'])
                    P.dma('pool', DMA(d['ot16'][j, dr], d['ot32'][j, dr]), reads=['ot32'], writes=['ot16'])
            swp32 = sb('p_swap32', [128, 128])
            swp = sb('p_swap', [128, 128], BF16)
            P.dma('sp', DMA(swp32[:], d['c_swap']), writes=['p_swap32'])
            P.op('dve', CP(swp[:], swp32[:]), reads=['p_swap32'], writes=['p_swap'])
            pmt = [sb('p_pmt%d' % i, [128, NSTEP, 128], BF16) for i in range(2)]
            pa = [sb('p_pa%d' % i, [128, NSTEP, 128], BF16) for i in range(2)]
            pb = [sb('p_pb%d' % i, [128, NSTEP, 128], BF16) for i in range(2)]
            idb = c['identb'][:].unsqueeze(1).broadcast_to([128, NSTEP, 128])
            swb = swp[:].unsqueeze(1).broadcast_to([128, NSTEP, 128])
            cnt = 0
            for j in js:
                for dr in range(2):
                    jd = j * 2 + dr
                    for g in range(128):
                        u = cnt % 2
                        cnt += 1
                        e1 = 'dve' if u == 0 else 'pool'
                        t1b = T1[:, jd, :, g:g + 1].broadcast_to([128, NSTEP, 128])
                        t2b = T2[:, jd, :, g:g + 1].broadcast_to([128, NSTEP, 128])
                        V(e1, TT(pa[u][:], idb, t1b, ALU.mult), ['p_T', 'k_identb'], ['p_pa%d' % u])
                        V(e1, TT(pb[u][:], swb, t2b, ALU.mult), ['p_T', 'p_swap'], ['p_pb%d' % u])
                        V(e1, TT(pmt[u][:], pa[u][:], pb[u][:], ALU.add), ['p_pa%d' % u, 'p_pb%d' % u], ['p_pmt%d' % u])
                        P.dma('sp', DMA(d['pm16'][j, dr, g], pmt[u][:].rearrange("p a b -> p (a b)")),
                              reads=['p_pmt%d' % u], writes=['pm16'])
            mkf, mkb = sb('p_mkf', [128, 128]), sb('p_mkb', [128, 128])
            sel, dsk, DS = sb('p_sel', [16, 128]), sb('p_dsk', [16, 2, 128]), sb('p_DS', [128, 2, 128])
            P.dma('sp', DMA(mkf[:], d['c_maskf']), writes=['p_mkf'])
            P.dma('sp', DMA(mkb[:], d['c_maskb']), writes=['p_mkb'])
            P.dma('sp', DMA(sel[:], d['c_sel']), writes=['p_sel'])
            for j in js:
                P.dma('sp', DMA(dsk[:, j, :], d['s5_d'][j].rearrange("(g q) -> q g", q=16), slow=True), writes=['p_dsk'])
                bk = self.next_bank()
                ps = self.banks[bk]
                P.op('pe', MM(ps[:, 0:128], sel[:], dsk[:, j, :]), reads=['p_sel', 'p_dsk'], writes=['bank%d' % bk])
                V('dve', CP(DS[:, j, :], ps[:, 0:128]), ['bank%d' % bk], ['p_DS'])
            NMB = 6
            wl = [sb('p_wl%d' % i, [128, 4, 128]) for i in range(NMB)]
            m1 = [sb('p_m1%d' % i, [128, 128]) for i in range(NMB)]
            m2 = [sb('p_m2%d' % i, [128, 128]) for i in range(NMB)]
            mo = [sb('p_mo%d' % i, [128, 128], BF16) for i in range(NMB)]
            cnt = 0
            for j in js:
                for g in range(128):
                    u = cnt % NMB
                    cnt += 1
                    nm = 'p_wl%d' % u
                    for dr in range(2):
                        P.dma('sp', DMA(wl[u][:, dr * 2, :], d['ry32'][j, dr, g].rearrange("(n k) -> n k", k=128)),
                              reads=['ry32'], writes=[nm])
                        P.dma('sp', DMA(wl[u][:, dr * 2 + 1, :], d['ot32'][j, dr, g].rearrange("(n k) -> n k", k=128)),
                              reads=['ot32'], writes=[nm])
                    bk = self.next_bank()
                    ps = self.banks[bk]
                    P.op('pe', [MM(ps[:, 0:128], wl[u][:, 0, :], wl[u][:, 1, :]), MM(ps[:, 128:256], wl[u][:, 2, :], wl[u][:, 3, :])],
                         reads=[nm], writes=['bank%d' % bk])
                    V('dve', TT(m1[u][:], ps[:, 0:128], mkf[:], ALU.mult), ['bank%d' % bk, 'p_mkf'], ['p_m1%d' % u])
                    V('dve', TT(m2[u][:], ps[:, 128:256], mkb[:], ALU.mult), ['bank%d' % bk, 'p_mkb'], ['p_m2%d' % u])
                    V('pool', TT(m1[u][:], m1[u][:], m2[u][:], ALU.add), ['p_m1%d' % u, 'p_m2%d' % u], ['p_m1%d' % u])
                    V('dve', STT(mo[u][:], c['ident'][:], DS[:, j, g:g + 1], m1[u][:], ALU.mult, ALU.add),
                      ['p_m1%d' % u, 'p_DS', 'k_ident'], ['p_mo%d' % u])
                    P.dma('act', DMA(d['mt16'][j, g], mo[u][:]), reads=['p_mo%d' % u], writes=['mt16'])
            P.barrier()
            P.flush()

    def s5_scan(self, j, src):
        nc, P, d, c = self.nc, self.P, self.d, self.c
        NG = 2
        with ExitStack() as es:
            sb = lambda n, s, dt=F32: self.sb(es, n, s, dt)
            xin2 = [sb('s_xin%d' % i, [128, 4, 8, 128]) for i in range(2)]
            xb = sb('s_xb', [128, 4, 8, 128], BF16)
            U = sb('s_U', [128, 8, 512], BF16)
            yt = sb('s_yt', [128, 4, 8, 128])
            S = [[sb('s_S%d%d' % (u, dr), [128, 512], BF16) for dr in range(2)] for u in range(NG)]
            tmpc = [sb('s_tc%d' % u, [128, 512]) for u in range(NG)]
            WA2 = [[sb('s_WA%d_%d' % (v, u), [128, 5, 128], BF16) for u in range(NG)] for v in range(2)]
            WP2 = [[sb('s_WP%d_%d' % (v, u), [128, 2, NSTEP, 128], BF16) for u in range(NG)] for v in range(2)]

            def load_w(pidx):
                v = pidx % 2
                cbp, gpp = divmod(pidx, 8 // NG)
                for u in range(NG):
                    g = cbp * 8 + gpp * NG + u
                    nm = 's_W%d_%d' % (v, u)
                    WAu, WPu = WA2[v][u], WP2[v][u]
                    P.dma('sp', DMA(WAu[:, 0, :], d['rt16'][j, 0, g].rearrange("(a b) -> a b", b=128)), reads=['rt16'], writes=[nm])
                    P.dma('sp', DMA(WAu[:, 1, :], d['rt16'][j, 1, g].rearrange("(a b) -> a b", b=128)), reads=['rt16'], writes=[nm])
                    P.dma('sp', DMA(WAu[:, 2, :], d['ot16'][j, 0, g].rearrange("(a b) -> a b", b=128)), reads=['ot16'], writes=[nm])
                    P.dma('sp', DMA(WAu[:, 3, :], d['ot16'][j, 1, g].rearrange("(a b) -> a b", b=128)), reads=['ot16'], writes=[nm])
                    P.dma('sp', DMA(WAu[:, 4, :], d['mt16'][j, g]), reads=['mt16'], writes=[nm])
                    for dr in range(2):
                        P.dma('sp', DMA(WPu[:, dr].rearrange("p a b -> p (a b)"), d['pm16'][j, dr, g]), reads=['pm16'], writes=[nm])
            NPAIR = 16 * (8 // NG)
            for s in range(self.nseq):
                xv = src[s].rearrange("(ct c t) (cb ch) -> cb c ct t ch", c=128, t=8, ch=128)
                yv = d['ys5'][s].rearrange("(ct c t) (cb ch) -> cb c ct t ch", c=128, t=8, ch=128)
                def load_x(cb, xv=xv):
                    for ct in range(4):
                        P.dma('sp', DMA(xin2[cb % 2][:, ct], xv[cb][:, ct]), reads=['src'], writes=['s_xin%d' % (cb % 2)])
                load_x(0)
                load_w(0)
                for cb in range(16):
                    xin = xin2[cb % 2]
                    if cb + 1 < 16:
                        load_x(cb + 1)
                    for ct in range(4):
                        P.op('act', ACTF(xb[:, ct].rearrange("p g (t q) -> p t g q", q=16),
                                         xin[:, ct].rearrange("p t (g q) -> p t g q", q=16), AF.Copy), reads=['s_xin%d' % (cb % 2)], writes=['s_xb'])
                    for g8 in range(8):
                        fns = []
                        for ct in range(4):
                            fns.append(TR(self.bank16[:, ct * 128:(ct + 1) * 128], xb[:, ct, g8, :], c['identb'][:]))
                        P.op('pe', fns, reads=['s_xb', 'k_identb'], writes=['bank16'])
                        if g8 % 2 == 0:
                            P.op('dve', CP(U[:, g8, :], self.bank16[:, 0:512]), reads=['bank16'], writes=['s_U%d' % g8])
                        else:
                            P.op('act', ACTF(U[:, g8, :], self.bank16[:, 0:512], AF.Copy), reads=['bank16'], writes=['s_U%d' % g8])
                    for gp in range(0, 8, NG):
                        pidx = cb * (8 // NG) + gp // NG
                        if pidx + 1 < NPAIR:
                            load_w(pidx + 1)
                        WA, WP = WA2[pidx % 2], WP2[pidx % 2]
                        wnm = ['s_W%d_%d' % (pidx % 2, u) for u in range(NG)]
                        chains = [(u, dr) for u in range(NG) for dr in range(2)]
                        cbank = {}
                        for (u, dr) in chains:
                            bk = self.next_bank()
                            cbank[(u, dr)] = bk
                            ps = self.banks[bk]
                            sn_ = 's_S%d%d' % (u, dr)
                            P.op('pe', MM(ps[:, :], WA[u][:, dr, :], U[:, gp + u, :]), reads=[wnm[u], 's_U%d' % (gp + u)], writes=['bank%d' % bk])
                            if dr == 0:
                                P.op('dve', [MEMSET(S[u][dr][:, 0:1], 0.0), CP(S[u][dr][:, 1:512], ps[:, 0:511])], reads=['bank%d' % bk], writes=[sn_])
                            else:
                                P.op('dve', [MEMSET(S[u][dr][:, 511:512], 0.0), CP(S[u][dr][:, 0:511], ps[:, 1:512])], reads=['bank%d' % bk], writes=[sn_])
                        for i in range(NSTEP):
                            sh = 1 << i
                            for (u, dr) in chains:
                                bk = cbank[(u, dr)]
                                ps = self.banks[bk]
                                sn_ = 's_S%d%d' % (u, dr)
                                if dr == 0:
                                    o, r = ps[:, sh:512], S[u][dr][:, 0:512 - sh]
                                else:
                                    o, r = ps[:, 0:512 - sh], S[u][dr][:, sh:512]
                                P.op('pe', MM(o, WP[u][:, dr, i, :], r), reads=[wnm[u], sn_], writes=['bank%d' % bk])
                            for (u, dr) in chains:
                                bk = cbank[(u, dr)]
                                ps = self.banks[bk]
                                sn_ = 's_S%d%d' % (u, dr)
                                if dr == 0:
                                    o, a = S[u][dr][:, sh:512], ps[:, sh:512]
                                else:
                                    o, a = S[u][dr][:, 0:512 - sh], ps[:, 0:512 - sh]
                                P.op('dve', TT(o, a, o, ALU.add), reads=['bank%d' % bk, sn_], writes=[sn_])
                        for u in range(NG):
                            g8 = gp + u
                            bk = self.next_bank()
                            ps = self.banks[bk]
                            fns = []
                            for ct in range(4):
                                o = ps[:, ct * 128:(ct + 1) * 128]
                                fns.append(MM(o, U[:, g8, ct * 128:(ct + 1) * 128], WA[u][:, 4, :], True, False))
                                fns.append(MM(o, S[u][0][:, ct * 128:(ct + 1) * 128], WA[u][:, 2, :], False, False))
                                fns.append(MM(o, S[u][1][:, ct * 128:(ct + 1) * 128], WA[u][:, 3, :], False, True))
                            P.op('pe', fns, reads=[wnm[u], 's_U%d' % g8, 's_S%d0' % u, 's_S%d1' % u], writes=['bank%d' % bk])
                            P.op('act', ACTF(yt[:, :, :, g8 * 16:(g8 + 1) * 16],
                                             ps[:, :].rearrange("p (a t q) -> p a t q", a=4, t=8, q=16), AF.Copy),
                                 reads=['bank%d' % bk], writes=['s_yt'])
                    for ct in range(4):
                        P.dma('sp', DMA(yv[cb][:, ct], yt[:, ct]), reads=['s_yt'], writes=['ys5'])
            P.barrier()
            P.flush()

    def kv_phase(self, src):
        nc, P, d, c = self.nc, self.P, self.d, self.c
        with ExitStack() as es:
            sb = lambda n, s, dt=F32: self.sb(es, n, s, dt)
            tm = [sb('v_tm%d' % i, [128, D]) for i in range(2)]
            tmb = [sb('v_tmb%d' % i, [128, D], BF16) for i in range(2)]
            Xb = sb('v_Xb', [128, NKC, BLK], BF16)
            W = [sb('v_W%d' % i, [128, NKC, 512], BF16) for i in range(2)]
            kq = sb('v_kq', [128, BLK])
            ksq = sb('v_ksq', [128, BLK])
            rs = sb('v_rs', [128, BLK])
            kh = sb('v_kh', [128, BLK])
            ta = sb('v_ta', [128, BLK])
            ktb = sb('v_ktb', [128, 4, BLK], BF16)
            vt = sb('v_vt', [128, 4, 512], BF16)
            cs, sn = sb('v_cos', [128, BLK]), sb('v_sin', [128, BLK])
            wv = d['wqkvb'].rearrange("(k p) n -> p k n", p=128)
            P.dma('sp', DMA(W[0][:], wv[:, :, 2048:2560]), reads=['wscr'], writes=['v_W0'])
            P.dma('sp', DMA(W[1][:], wv[:, :, 2560:3072]), reads=['wscr'], writes=['v_W1'])
            tcount = 0
            for s in range(self.nseq):
                for b in range(self.nblk):
                    t0 = b * BLK
                    P.dma('sp', DMA(cs[:], d['c_cos'][:, t0:t0 + BLK]), writes=['v_cos'])
                    P.dma('sp', DMA(sn[:], d['c_sin'][:, t0:t0 + BLK]), writes=['v_sin'])
                    for tt in range(4):
                        u = tcount % 2
                        tcount += 1
                        P.dma('sp', DMA(tm[u][:], src[s, t0 + tt * 128:t0 + (tt + 1) * 128, :]), reads=['src'], writes=['v_tm%d' % u])
                        P.op('act', ACTF(tmb[u][:], tm[u][:], AF.Copy), reads=['v_tm%d' % u], writes=['v_tmb%d' % u])
                        for k4 in range(4):
                            fns = [TR(self.bank16[:, i * 128:(i + 1) * 128], tmb[u][:, (k4 * 4 + i) * 128:(k4 * 4 + i + 1) * 128], c['identb'][:])
                                   for i in range(4)]
                            P.op('pe', fns, reads=['v_tmb%d' % u, 'k_identb'], writes=['bank16'])
                            P.op('dve', CP(Xb[:, k4 * 4:(k4 + 1) * 4, tt * 128:(tt + 1) * 128],
                                           self.bank16[:, 0:512].rearrange("p (a b) -> p a b", a=4)),
                                 reads=['bank16'], writes=['v_Xb'])
                    for h in range(4):
                        bk = self.next_bank()
                        ps = self.banks[bk]
                        P.op('pe', [MM(ps[:, :], W[0][:, kc, h * 128:(h + 1) * 128], Xb[:, kc, :], kc == 0, kc == NKC - 1) for kc in range(NKC)],
                             reads=['v_W0', 'v_Xb'], writes=['bank%d' % bk])
                        self.qk_norm_rope(ps, bk, ktb[:, h, :], 'v_ktb', 1, kq, ksq, rs, kh, ta, cs, sn, 'v')
                    for h in range(4):
                        P.dma('sp', DMA(d['kts'][s, h, :, t0:t0 + BLK], ktb[:, h, :]), reads=['v_ktb'], writes=['kts'])
                    for tt in range(4):
                        bk = self.next_bank()
                        ps = self.banks[bk]
                        P.op('pe', [MM(ps[:, :], Xb[:, kc, tt * 128:(tt + 1) * 128], W[1][:, kc, :], kc == 0, kc == NKC - 1) for kc in range(NKC)],
                             reads=['v_W1', 'v_Xb'], writes=['bank%d' % bk])
                        P.op('act', ACTF(vt[:, tt, :], ps[:, :], AF.Copy),
                             reads=['bank%d' % bk], writes=['v_vt'])
                    for tt in range(4):
                        sc = b * 4 + tt
                        P.dma('sp', DMA(d['vs'][s, :, :, sc, :].rearrange("h p e -> p h e"),
                                        vt[:, tt, :].rearrange("p (h e) -> p h e", h=4)), reads=['v_vt'], writes=['vs'])
            P.barrier()
            P.flush()

    def qk_norm_rope(self, ps, bk, out_bf, out_name, which, kq, ksq, rs, kh, ta, cs, sn, pfx):
        P, c = self.P, self.c
        bn = 'bank%d' % bk
        n = lambda x: pfx + '_' + x
        P.op('act', ACTF(kq[:], ps[:, :], AF.Copy), reads=[bn], writes=[n('kq')])
        P.op('pool', TT(ksq[:], kq[:], kq[:], ALU.mult), reads=[n('kq')], writes=[n('ksq')])
        b2 = self.next_bank()
        ps2 = self.banks[b2]
        P.op('pe', MM(ps2[:, :], c['ones_rms'][:], ksq[:]), reads=[n('ksq'), 'k_ones_rms'], writes=['bank%d' % b2])
        P.op('act', ACTF(rs[:], ps2[:, :], AF.Sqrt, bias=c['eps'][:, 1:2]), reads=['bank%d' % b2, 'k_eps'], writes=[n('rs')])
        P.op('dve', lambda e, o=rs[:]: e.reciprocal(out=o, in_=o), reads=[n('rs')], writes=[n('rs')])
        P.op('dve', STT(kh[:], kq[:], c['qkn'][:, which:which + 1], rs[:], ALU.mult, ALU.mult), reads=[n('kq'), n('rs'), 'k_qkn'], writes=[n('kh')])
        b3 = self.next_bank()
        ps3 = self.banks[b3]
        P.op('pe', MM(ps3[:, :], c['rot'][:], kh[:]), reads=[n('kh'), 'k_rot'], writes=['bank%d' % b3])
        P.op('pool', TT(ta[:], kh[:], cs[:], ALU.mult), reads=[n('kh'), n('cos')], writes=[n('ta')])
        P.op('dve', TT(rs[:], ps3[:, :], sn[:], ALU.mult), reads=['bank%d' % b3, n('sin')], writes=[n('rs')])
        P.op('pool', TT(out_bf, ta[:], rs[:], ALU.add), reads=[n('ta'), n('rs')], writes=[out_name])

    def block_phase(self, l, src, dst):
        nc, P, d, c = self.nc, self.P, self.d, self.c
        kind = l % 3
        j = l // 3
        with ExitStack() as es:
            sb = lambda n, s, dt=F32: self.sb(es, n, s, dt)
            NX = 2 if kind != 2 else 1
            Xs = [sb('b_X%d' % i, [128, NKC, XW]) for i in range(NX)]
            XRs = [['b_X%d_%d' % (i, kc) for kc in range(NKC)] for i in range(NX)]
            Xb = sb('b_Xb', [128, NKC, BLK], BF16)
            HT = sb('b_HT', [128, 64 if kind == 2 else 32, BLK], BF16)
            tm = [sb('b_tm%d' % i, [128, D]) for i in range(2)]
            W = [sb('b_W%d' % i, [128, NKC, 512], BF16) for i in range(2)]
            st = [sb('b_st%d' % i, [128, BLK]) for i in range(6)]
            sq = [sb('b_sq%d' % i, [128, BLK]) for i in range(2)]
            sqb = [sb('b_sqb%d' % i, [128, BLK], BF16) for i in range(2)]
            pt = [sb('b_pt%d' % i, [128, BLK], BF16) for i in range(8)] if kind == 2 else None
            self._wrr = 0
            self._tmrr = 0
            self._sqrr = 0
            self._ptrr = 0

            def wload(view_fn):
                u = self._wrr
                self._wrr ^= 1
                view_fn(W[u], 'b_W%d' % u)
                return W[u], 'b_W%d' % u

            def mm16(lhs_fn, rhs_fn, reads, nk=NKC):
                bk = self.next_bank()
                ps = self.banks[bk]
                P.op('pe', [MM(ps[:, :], lhs_fn(kc), rhs_fn(kc), kc == 0, kc == nk - 1) for kc in range(nk)],
                     reads=reads, writes=['bank%d' % bk])
                return ps, 'bank%d' % bk

            def layer_norm(which, XM, XR, mid=None):
                g_ap = lambda kc: c['lnp'][:, which * 2, l, kc:kc + 1]
                b_ap = lambda kc: c['lnp'][:, which * 2 + 1, l, kc:kc + 1]
                bm = self.next_bank()
                bq = self.next_bank()
                psm, psq = self.banks[bm], self.banks[bq]
                for kc in range(NKC):
                    u = self._sqrr
                    self._sqrr ^= 1
                    P.op('pool', TT(sqb[u][:], XM(kc), XM(kc), ALU.mult), reads=[XR[kc]], writes=['b_sqb%d' % u])
                    P.op('pe', [MM(psm[:, :], c['ones_ln'][:], XM(kc), kc == 0, kc == NKC - 1),
                                MM(psq[:, :], c['onesb'][:], sqb[u][:], kc == 0, kc == NKC - 1)],
                         reads=[XR[kc], 'b_sqb%d' % u, 'k_ones_ln', 'k_onesb'], writes=['bank%d' % bm, 'bank%d' % bq])
                mean, var, rstd, nmr = st[0], st[1], st[2], st[3]
                P.op('act', ACTF(mean[:], psm[:, :], AF.Copy), reads=['bank%d' % bm], writes=['b_st0'])
                P.op('dve', TT(var[:], mean[:], mean[:], ALU.mult), reads=['b_st0'], writes=['b_st1'])
                P.op('dve', STT(var[:], psq[:, :], 1.0 / D, var[:], ALU.mult, ALU.subtract), reads=['bank%d' % bq, 'b_st1'], writes=['b_st1'])
                P.op('act', ACTF(rstd[:], var[:], AF.Sqrt, bias=c['eps'][:, 0:1]), reads=['b_st1', 'k_eps'], writes=['b_st2'])
                P.op('dve', lambda e, o=rstd[:]: e.reciprocal(out=o, in_=o), reads=['b_st2'], writes=['b_st2'])
                P.op('dve', STT(nmr[:], mean[:], -1.0, rstd[:], ALU.mult, ALU.mult), reads=['b_st0', 'b_st2'], writes=['b_st3'])
                if mid is not None:
                    mid()
                for kc in range(NKC):
                    u = self._sqrr
                    self._sqrr ^= 1
                    e1 = 'dve' if kc % 2 == 0 else 'pool'
                    P.op(e1, TT(sq[u][:], XM(kc), rstd[:], ALU.mult), reads=[XR[kc], 'b_st2'], writes=['b_sq%d' % u])
                    P.op(e1, TT(sq[u][:], sq[u][:], nmr[:], ALU.add), reads=['b_sq%d' % u, 'b_st3'], writes=['b_sq%d' % u])
                    P.op('act', [ACTF(XM(kc), sq[u][:], AF.Identity, scale=g_ap(kc), bias=b_ap(kc)),
                                 ACTF(Xb[:, kc, :], sq[u][:], AF.Identity, scale=g_ap(kc), bias=b_ap(kc))],
                         reads=['b_sq%d' % u, 'k_lnp'], writes=[XR[kc], 'b_Xb%d' % kc])

            dummy = sb('b_dummy', [128, 2])
            self.pool_t = [[sb('b_pl%d%d' % (e_, i_), [128, XW]) for i_ in range(2)] for e_ in range(2)] if kind == 1 else None
            self.st2 = [sb('b_sx%d' % i_, [128, BLK]) for i_ in range(5)] if kind == 2 else None
            self.dacc = [sb('b_dacc%d' % i_, [128, BLK]) for i_ in range(2)] if kind == 2 else None
            if kind == 2:
                self.onesbs = sb('b_onesbs', [128, 128], BF16)
                P.op('dve', TS(self.onesbs[:], c['onesb'][:], 1.0 / 128, ALU.mult), reads=['k_onesb'], writes=['b_onesbs'])
            fence_names = ['b_HT0', 'b_HT1', 'b_HT2', 'b_HT3', 'b_OT', 'b_ga', 'a_cos', 'a_sin'] + ['b_QT%d' % h for h in range(16)]

            def fence():
                P.op('pool', MEMSET(dummy[:, 0:1], 0.0), reads=[], writes=fence_names)

            XBR = ['b_Xb%d' % kc for kc in range(NKC)]
            blocks = [(s_, b_) for s_ in range(self.nseq) for b_ in range(self.nblk)]

            def emit_load(idx):
                s, b = blocks[idx]
                X, XR = Xs[idx % NX], XRs[idx % NX]
                t0 = b * BLK
                for tt in range(4):
                    u = self._tmrr
                    self._tmrr ^= 1
                    P.dma('sp', DMA(tm[u][:], src[s, t0 + tt * 128:t0 + (tt + 1) * 128, :]), reads=['src'], writes=['b_tm%d' % u])
                    for k4 in range(4):
                        bk = self.next_bank()
                        ps = self.banks[bk]
                        P.op('pe', [TR(ps[:, i * 128:(i + 1) * 128], tm[u][:, (k4 * 4 + i) * 128:(k4 * 4 + i + 1) * 128], c['ident'][:])
                                    for i in range(4)], reads=['b_tm%d' % u, 'k_ident'], writes=['bank%d' % bk])
                        dst_ap = X[:, k4 * 4:(k4 + 1) * 4, PAD + tt * 128:PAD + (tt + 1) * 128]
                        src_ap = ps[:, :].rearrange("p (a b) -> p a b", a=4)
                        if k4 % 2 == 0:
                            P.op('dve', CP(dst_ap, src_ap), reads=['bank%d' % bk], writes=XR[k4 * 4:(k4 + 1) * 4])
                        else:
                            P.op('act', ACTF(dst_ap, src_ap, AF.Copy), reads=['bank%d' % bk], writes=XR[k4 * 4:(k4 + 1) * 4])

            emit_load(0)
            for idx, (s, b) in enumerate(blocks):
                if True:
                    t0 = b * BLK
                    X, XR = Xs[idx % NX], XRs[idx % NX]
                    XM = lambda kc, X=X: X[:, kc, PAD:PAD + BLK]
                    fence()
                    if kind == 0:
                        self.mixer_s5(j, s, t0, X, XM, Xb, HT, tm, W, st, wload, mm16, XR)
                    elif kind == 1:
                        self.mixer_pool(s, t0, src, X, XM, HT, tm, W, st, sq, wload, mm16, XR)
                    else:
                        self.mixer_attn(s, t0, X, XM, Xb, HT, W, st, pt, wload, mm16, XR, XBR)
                    layer_norm(0, XM, XR)
                    fence()
                    w1v = d['w1b'][l].rearrange("(k p) f -> p k f", p=128)
                    w2v = d['w2b'][l].rearrange("(k p) n -> p k n", p=128)
                    for half in range(2):
                        for fg in range(half * 8, half * 8 + 8):
                            Wt, wn = wload(lambda buf, nm, fg=fg: P.dma('sp', DMA(buf[:], w1v[:, :, fg * 512:(fg + 1) * 512]), reads=['wscr'], writes=[nm]))
                            for fq in range(4):
                                ps, bn = mm16(lambda kc, Wt=Wt, fq=fq: Wt[:, kc, fq * 128:(fq + 1) * 128], lambda kc: Xb[:, kc, :], [wn] + XBR)
                                u = self._sqrr
                                self._sqrr ^= 1
                                P.op('act', ACTF(sq[u][:], ps[:, :], AF.Relu), reads=[bn], writes=['b_sq%d' % u])
                                li = (fg - half * 8) * 4 + fq
                                P.op('dve' if fq % 2 == 0 else 'pool', TT(HT[:, li, :], sq[u][:], sq[u][:], ALU.mult),
                                     reads=['b_sq%d' % u], writes=['b_HT%d' % (li // 16)])
                        for dg in range(4):
                            for fsl in range(2):
                                fs = half * 2 + fsl
                                Wt, wn = wload(lambda buf, nm, dg=dg, fs=fs: P.dma('sp', DMA(buf[:], w2v[:, fs * 16:(fs + 1) * 16, dg * 512:(dg + 1) * 512]),
                                                                                  reads=['wscr'], writes=[nm]))
                                for dq in range(4):
                                    kc_o = dg * 4 + dq
                                    ps, bn = mm16(lambda fc, Wt=Wt, dq=dq: Wt[:, fc, dq * 128:(dq + 1) * 128],
                                                  lambda fc, fsl=fsl: HT[:, fsl * 16 + fc, :], [wn, 'b_HT%d' % fsl])
                                    if half == 0 and fsl == 0:
                                        P.op('dve', STT(XM(kc_o), XM(kc_o), ALPHA, ps[:, :], ALU.mult, ALU.add), reads=[bn, XR[kc_o]], writes=[XR[kc_o]])
                                    else:
                                        P.op('dve', TT(XM(kc_o), XM(kc_o), ps[:, :], ALU.add), reads=[bn, XR[kc_o]], writes=[XR[kc_o]])
                    nxt = (lambda idx=idx: emit_load(idx + 1)) if (NX == 2 and idx + 1 < len(blocks)) else None
                    layer_norm(1, XM, XR, mid=nxt)
                    for tt in range(4):
                        u = self._tmrr
                        self._tmrr ^= 1
                        for k4 in range(4):
                            bk = self.next_bank()
                            ps = self.banks[bk]
                            P.op('pe', [TR(ps[:, i * 128:(i + 1) * 128], X[:, k4 * 4 + i, PAD + tt * 128:PAD + (tt + 1) * 128], c['ident'][:])
                                        for i in range(4)], reads=XR[k4 * 4:(k4 + 1) * 4] + ['k_ident'], writes=['bank%d' % bk])
                            if k4 % 2 == 0:
                                P.op('dve', CP(tm[u][:, k4 * 512:(k4 + 1) * 512], ps[:, :]), reads=['bank%d' % bk], writes=['b_tm%d' % u])
                            else:
                                P.op('act', ACTF(tm[u][:, k4 * 512:(k4 + 1) * 512], ps[:, :], AF.Copy), reads=['bank%d' % bk], writes=['b_tm%d' % u])
                        P.dma('pool', DMA(dst[s, t0 + tt * 128:t0 + (tt + 1) * 128, :], tm[u][:]), reads=['b_tm%d' % u], writes=['dst'])
                    if NX == 1 and idx + 1 < len(blocks):
                        emit_load(idx + 1)
            P.barrier()
            P.flush()

    def mixer_s5(self, j, s, t0, X, XM, Xb, HT, tm, W, st, wload, mm16, XR):
        P, d, c = self.P, self.d, self.c
        GT = HT
        ga = HT[:, 16:20, :].rearrange("p a b -> p (a b)")
        for tt in range(4):
            u = self._tmrr
            self._tmrr ^= 1
            y = tm[u]
            yn = 'b_tm%d' % u
            P.dma('sp', DMA(y[:], d['ys5'][s, t0 + tt * 128:t0 + (tt + 1) * 128, :]), reads=['ys5'], writes=[yn])
            u2 = self._tmrr
            tq = tm[u2]
            tn = 'b_tm%d' % u2
            P.op('pool', TT(tq[:], y[:], y[:], ALU.mult), reads=[yn], writes=[tn])
            P.op('pool', TS(tq[:], tq[:], 0.044715, ALU.mult, 1.0, ALU.add), reads=[tn], writes=[tn])
            P.op('dve', TT(tq[:], tq[:], y[:], ALU.mult), reads=[tn, yn], writes=[tn])
            P.op('act', ACTF(tq[:], tq[:], AF.Sigmoid, scale=GC), reads=[tn], writes=[tn])
            P.op('dve', TT(ga, tq[:], y[:], ALU.mult), reads=[tn, yn], writes=['b_ga'])
            for k4 in range(4):
                P.op('pe', [TR(self.bank16[:, i * 128:(i + 1) * 128], ga[:, (k4 * 4 + i) * 128:(k4 * 4 + i + 1) * 128], c['identb'][:]) for i in range(4)],
                     reads=['b_ga', 'k_identb'], writes=['bank16'])
                P.op('dve', CP(GT[:, k4 * 4:(k4 + 1) * 4, tt * 128:(tt + 1) * 128], self.bank16[:, 0:512].rearrange("p (a b) -> p a b", a=4)),
                     reads=['bank16'], writes=['b_HT0'])
        wo = d['woutb'][j].rearrange("(k p) n -> p k n", p=128)
        wg = d['wgateb'][j].rearrange("(k p) n -> p k n", p=128)
        for dg in range(4):
            Wo, won = wload(lambda buf, nm, dg=dg: P.dma('sp', DMA(buf[:], wo[:, :, dg * 512:(dg + 1) * 512]), reads=['wscr'], writes=[nm]))
            Wg, wgn = wload(lambda buf, nm, dg=dg: P.dma('sp', DMA(buf[:], wg[:, :, dg * 512:(dg + 1) * 512]), reads=['wscr'], writes=[nm]))
            for dq in range(4):
                kc_o = dg * 4 + dq
                pso, bo = mm16(lambda kc, Wo=Wo, dq=dq: Wo[:, kc, dq * 128:(dq + 1) * 128], lambda kc: GT[:, kc, :], [won, 'b_HT0'])
                psg, bg = mm16(lambda kc, Wg=Wg, dq=dq: Wg[:, kc, dq * 128:(dq + 1) * 128], lambda kc: GT[:, kc, :], [wgn, 'b_HT0'])
                sg = st[4 + (dq % 2)]
                sgn = 'b_st%d' % (4 + (dq % 2))
                P.op('act', ACTF(sg[:], psg[:, :], AF.Sigmoid), reads=[bg], writes=[sgn])
                P.op('dve', TT(sg[:], pso[:, :], sg[:], ALU.mult), reads=[bo, sgn], writes=[sgn])
                P.op('dve', STT(XM(kc_o), XM(kc_o), ALPHA, sg[:], ALU.mult, ALU.add), reads=[sgn, XR[kc_o]], writes=[XR[kc_o]])

    def mixer_pool(self, s, t0, src, X, XM, HT, tm, W, st, sq, wload, mm16, XR):
        P, d, c = self.P, self.d, self.c
        u = self._tmrr
        self._tmrr ^= 1
        hb = tm[u]
        hn = 'b_tm%d' % u
        P.op('pool', MEMSET(hb[0:16, :], 0.0), reads=[], writes=[hn])
        if t0 > 0:
            P.dma('sp', DMA(hb[0:8, :], src[s, t0 - 8:t0, :]), reads=['src'], writes=[hn])
        if t0 + BLK < L:
            P.dma('sp', DMA(hb[8:16, :], src[s, t0 + BLK:t0 + BLK + 8, :]), reads=['src'], writes=[hn])
        for k4 in range(4):
            bk = self.next_bank()
            ps = self.banks[bk]
            P.op('pe', [TR(ps[:, i * 16:(i + 1) * 16], hb[0:16, (k4 * 4 + i) * 128:(k4 * 4 + i + 1) * 128], c['ident'][0:16, 0:16]) for i in range(4)],
                 reads=[hn, 'k_ident'], writes=['bank%d' % bk])
            pv = ps[:, 0:64].rearrange("p (a b) -> p a b", a=4)
            P.op('dve', [CP(X[:, k4 * 4:(k4 + 1) * 4, 0:PAD], pv[:, :, 0:8]), CP(X[:, k4 * 4:(k4 + 1) * 4, PAD + BLK:XW], pv[:, :, 8:16])],
                 reads=['bank%d' % bk], writes=XR[k4 * 4:(k4 + 1) * 4])
        ic = tm[self._tmrr]
        icn = 'b_tm%d' % self._tmrr
        self._tmrr ^= 1
        P.dma('sp', DMA(ic[:, 0:4 * BLK].rearrange("p (a b) -> p a b", a=4), d['c_invc'][:, :, t0:t0 + BLK]), writes=[icn])
        PT = HT
        T = [st[0], st[1]]
        for kc in range(NKC):
            gi = kc // 4
            eng = 'dve' if kc % 2 == 0 else 'pool'
            a = [st[(kc % 2) * 2], st[(kc % 2) * 2 + 1]]
            an = ['b_st%d' % ((kc % 2) * 2), 'b_st%d' % ((kc % 2) * 2 + 1)]
            xs = X[:, kc, :]
            w = (2, 4, 8, 16)[gi]
            j0 = PAD - w // 2
            e_i = kc % 2
            ta_, tb_ = self.pool_t[e_i][0], self.pool_t[e_i][1]
            tan, tbn = 'b_pl%d0' % e_i, 'b_pl%d1' % e_i
            acc, accn = a[0], an[0]
            if w == 2:
                P.op(eng, TT(acc[:], xs[:, j0:j0 + BLK], xs[:, j0 + 1:j0 + 1 + BLK], ALU.add), reads=[XR[kc]], writes=[accn])
            else:
                P.op(eng, TT(ta_[:, 0:XW - 1], xs[:, 0:XW - 1], xs[:, 1:XW], ALU.add), reads=[XR[kc]], writes=[tan])
                cur, curn, oth, othn, wd, step = ta_, tan, tb_, tbn, XW - 1, 2
                while step * 2 < w:
                    nwd = wd - step
                    P.op(eng, TT(oth[:, 0:nwd], cur[:, 0:nwd], cur[:, step:step + nwd], ALU.add), reads=[curn], writes=[othn])
                    cur, curn, oth, othn, wd, step = oth, othn, cur, curn, nwd, step * 2
                P.op(eng, TT(acc[:], cur[:, j0:j0 + BLK], cur[:, j0 + step:j0 + step + BLK], ALU.add), reads=[curn], writes=[accn])
            P.op(eng, TT(acc[:], acc[:], ic[:, gi * BLK:(gi + 1) * BLK], ALU.mult), reads=[accn, icn], writes=[accn])
            P.op(eng, TT(PT[:, kc, :], acc[:], XM(kc), ALU.subtract), reads=[accn, XR[kc]], writes=['b_HT0'])
        pw = d['poolwb'].rearrange("g (k p) n -> g p k n", p=128)
        for gi in range(4):
            Wt, wn = wload(lambda buf, nm, gi=gi: P.dma('sp', DMA(buf[:, 0:4, :], pw[gi]), reads=['wscr'], writes=[nm]))
            for dq in range(4):
                kc_o = gi * 4 + dq
                ps, bn = mm16(lambda k, Wt=Wt, dq=dq: Wt[:, k, dq * 128:(dq + 1) * 128], lambda k, gi=gi: PT[:, gi * 4 + k, :], [wn, 'b_HT0'], nk=4)
                sg = st[4 + (dq % 2)]
                sgn = 'b_st%d' % (4 + (dq % 2))
                P.op('act', ACTF(sg[:], ps[:, :], AF.Copy, scale=c['pscale'][:, kc_o:kc_o + 1]), reads=[bn, 'k_pscale'], writes=[sgn])
                P.op('dve', STT(XM(kc_o), XM(kc_o), ALPHA, sg[:], ALU.mult, ALU.add), reads=[sgn, XR[kc_o]], writes=[XR[kc_o]])

    def mixer_attn(self, s, t0, X, XM, Xb, HT, W, st, pt, wload, mm16, XR, XBR):
        P, d, c = self.P, self.d, self.c
        QT = HT
        OT = HT
        cs, sn = HT[:, 40:42, :].rearrange("p a b -> p (a b)").bitcast(F32), HT[:, 42:44, :].rearrange("p a b -> p (a b)").bitcast(F32)
        P.dma('sp', DMA(cs, d['c_cos'][:, t0:t0 + BLK]), writes=['a_cos'])
        P.dma('sp', DMA(sn, d['c_sin'][:, t0:t0 + BLK]), writes=['a_sin'])
        for kc in range(NKC):
            P.op('pool' if kc % 2 else 'act', CP(Xb[:, kc, :], XM(kc)) if kc % 2 else ACTF(Xb[:, kc, :], XM(kc), AF.Copy),
                 reads=[XR[kc]], writes=[XBR[kc]])
        wv = d['wqkvb'].rearrange("(k p) n -> p k n", p=128)
        Tsets = [[(st[i_], 'b_st%d' % i_) for i_ in range(5)], [(self.st2[i_], 'b_sx%d' % i_) for i_ in range(5)]]

        class _N:
            pass
        qps = {}
        wcur = {}

        def emit_q(h):
            hg, hq = divmod(h, 4)
            if hq == 0:
                wcur['W'] = wload(lambda buf, nm, hg=hg: P.dma('sp', DMA(buf[:], wv[:, :, hg * 512:(hg + 1) * 512]), reads=['wscr'], writes=[nm]))
            Wt, wn = wcur['W']
            qps[h] = mm16(lambda kc, Wt=Wt, hq=hq: Wt[:, kc, hq * 128:(hq + 1) * 128], lambda kc: Xb[:, kc, :], [wn] + XBR)
        emit_q(0)
        emit_q(1)
        for h in range(16):
            if h + 2 < 16:
                emit_q(h + 2)
            ps, bn = qps.pop(h)
            self._qk(ps, bn, QT[:, h, :], 'b_QT%d' % h, 0, Tsets[h % 2], cs, sn)
        scale = 128 ** -0.5
        for kvh in range(4):
            u = self._wrr
            self._wrr ^= 1
            KV = W[u]
            kvn = 'b_W%d' % u
            KT = KV[:, 0:8, :].rearrange("p a b -> p (a b)")
            Vv = KV[:, 8:16, :].rearrange("p a b -> p (a b)").rearrange("p (sc e) -> p sc e", e=128)
            P.dma('sp', DMA(KT, d['kts'][s, kvh]), reads=['kts'], writes=[kvn])
            P.dma('sp', DMA(Vv, d['vs'][s, kvh]), reads=['vs'], writes=[kvn])
            for hq in range(4):
                h = kvh * 4 + hq
                bo = self.next_bank()
                bd = self.next_bank()
                pso, psd = self.banks[bo], self.banks[bd]
                dacc, daccn = self.dacc[h % 2], 'b_dacc%d' % (h % 2)
                sbank = {}

                def emitS(sc, h=h, bo=bo, bd=bd, KT=KT, kvn=kvn):
                    bs = self.next_bank()
                    while bs in (bo, bd):
                        bs = self.next_bank()
                    P.op('pe', MM(self.banks[bs][:, :], KT[:, sc * 128:(sc + 1) * 128], QT[:, h, :]), reads=[kvn, 'b_QT%d' % h], writes=['bank%d' % bs])
                    sbank[sc] = bs
                emitS(0)
                emitS(1)
                emitS(2)
                for sc in range(32):
                    bs = sbank[sc]
                    pss = self.banks[bs]
                    pu = self._ptrr
                    self._ptrr = (self._ptrr + 1) % 8
                    P.op('act', ACTF(pt[pu][:], pss[:, :], AF.Exp, scale=scale), reads=['bank%d' % bs], writes=['b_pt%d' % pu])
                    if sc + 3 < 32:
                        emitS(sc + 3)
                    if sc % 2 == 0:
                        P.op('pe', [MM(pso[:, :], Vv[:, sc, :], pt[pu][:], sc == 0, sc == 31), MM(psd[:, :], self.onesbs[:], pt[pu][:], sc == 0, False)],
                             reads=[kvn, 'b_pt%d' % pu, 'b_onesbs'], writes=['bank%d' % bo, 'bank%d' % bd])
                    else:
                        P.op('pe', MM(pso[:, :], Vv[:, sc, :], pt[pu][:], sc == 0, sc == 31), reads=[kvn, 'b_pt%d' % pu], writes=['bank%d' % bo])
                        if sc == 1:
                            P.op('dve', CP(dacc[:], pt[pu][:]), reads=['b_pt%d' % pu], writes=[daccn])
                        else:
                            P.op('dve', TT(dacc[:], dacc[:], pt[pu][:], ALU.add), reads=['b_pt%d' % pu, daccn], writes=[daccn])
                P.op('pe', MM(psd[:, :], c['ones_rms'][:], dacc[:], False, True), reads=[daccn, 'k_ones_rms'], writes=['bank%d' % bd])
                rd = st[5]
                P.op('dve', lambda e, o=rd[:], a=psd[:, :]: e.reciprocal(out=o, in_=a), reads=['bank%d' % bd], writes=['b_st5'])
                P.op('dve', STT(OT[:, 16 + h, :], pso[:, :], 1.0 / 128, rd[:], ALU.mult, ALU.mult), reads=['bank%d' % bo, 'b_st5'], writes=['b_OT'])
        wo = d['wob'].rearrange("(k p) n -> p k n", p=128)
        for dg in range(4):
            Wt, wn = wload(lambda buf, nm, dg=dg: P.dma('sp', DMA(buf[:], wo[:, :, dg * 512:(dg + 1) * 512]), reads=['wscr'], writes=[nm]))
            for dq in range(4):
                kc_o = dg * 4 + dq
                ps, bn = mm16(lambda kc, Wt=Wt, dq=dq: Wt[:, kc, dq * 128:(dq + 1) * 128], lambda kc: OT[:, 16 + kc, :], [wn, 'b_OT'])
                P.op('dve', STT(XM(kc_o), XM(kc_o), ALPHA, ps[:, :], ALU.mult, ALU.add), reads=[bn, XR[kc_o]], writes=[XR[kc_o]])

    def _qk(self, ps, bn, out_bf, out_name, which, T, cs, sn):
        P, c = self.P, self.c
        (kq, nkq), (ksq, nksq), (rs, nrs), (kh, nkh), (ta, nta) = T
        P.op('act', ACTF(kq[:], ps[:, :], AF.Copy), reads=[bn], writes=[nkq])
        P.op('pool', TT(ksq[:], kq[:], kq[:], ALU.mult), reads=[nkq], writes=[nksq])
        b2 = self.next_bank()
        ps2 = self.banks[b2]
        P.op('pe', MM(ps2[:, :], c['ones_rms'][:], ksq[:]), reads=[nksq, 'k_ones_rms'], writes=['bank%d' % b2])
        P.op('act', ACTF(rs[:], ps2[:, :], AF.Sqrt, bias=c['eps'][:, 1:2]), reads=['bank%d' % b2, 'k_eps'], writes=[nrs])
        P.op('dve', lambda e, o=rs[:]: e.reciprocal(out=o, in_=o), reads=[nrs], writes=[nrs])
        P.op('dve', STT(kh[:], kq[:], c['qkn'][:, which:which + 1], rs[:], ALU.mult, ALU.mult), reads=[nkq, nrs, 'k_qkn'], writes=[nkh])
        b3 = self.next_bank()
        ps3 = self.banks[b3]
        P.op('pe', MM(ps3[:, :], c['rot'][:], kh[:]), reads=[nkh, 'k_rot'], writes=['bank%d' % b3])
        P.op('pool', TT(ta[:], kh[:], cs, ALU.mult), reads=[nkh, 'a_cos'], writes=[nta])
        P.op('dve', TT(rs[:], ps3[:, :], sn, ALU.mult), reads=['bank%d' % b3, 'a_sin'], writes=[nrs])
        P.op('pool', TT(out_bf, ta[:], rs[:], ALU.add), reads=[nta, nrs], writes=[out_name])


def make_consts():
    bf = ml_dtypes.bfloat16
    k = {}
    k['c_ident'] = np.eye(128, dtype=np.float32)
    k['c_identb'] = np.eye(128, dtype=np.float32).astype(bf)
    sw = np.zeros((128, 128), np.float32)
    for i in range(128):
        sw[i, (i + 64) % 128] = 1.0
    k['c_swap'] = sw
    R = np.zeros((128, 128), np.float32)
    for m in range(128):
        if m % 64 < 32:
            R[m, m + 32] = -1.0
        else:
            R[m, m - 32] = 1.0
    k['c_rot'] = np.ascontiguousarray(R.T)
    k['c_ones_ln'] = np.full((128, 128), 1.0 / D, np.float32)
    k['c_ones_rms'] = np.full((128, 128), 1.0 / 128, np.float32)
    k['c_onesb'] = np.ones((128, 128), np.float32).astype(bf)
    tau = np.arange(128) // 16
    k['c_maskf'] = (tau[None, :] >= tau[:, None]).astype(np.float32)
    k['c_maskb'] = (tau[None, :] <= tau[:, None]).astype(np.float32)
    sel = np.zeros((16, 128), np.float32)
    for q in range(16):
        sel[q, q::16] = 1.0
    k['c_sel'] = sel
    t = np.arange(L)
    row = (t // 64).astype(np.float32)
    col = (t % 64).astype(np.float32)
    inv_freq = np.power(np.float32(10000.0), -np.arange(32, dtype=np.float32) / np.float32(32)).astype(np.float32)
    cos = np.zeros((128, L), np.float32)
    sin = np.zeros((128, L), np.float32)
    for hd in range(128):
        f = hd % 32
        pos = row if hd < 64 else col
        ang = (pos * inv_freq[f]).astype(np.float32)
        cos[hd] = np.cos(ang)
        sin[hd] = np.sin(ang)
    k['c_cos'] = cos
    k['c_sin'] = sin
    invc = np.zeros((4, L), np.float32)
    for gi, w in enumerate((2, 4, 8, 16)):
        lo = np.clip(t - w // 2, 0, L)
        hi = np.clip(t + w // 2, 0, L)
        invc[gi] = 1.0 / (hi - lo).astype(np.float32)
    k['c_invc'] = np.ascontiguousarray(np.broadcast_to(invc[None], (128, 4, L))).astype(np.float32)
    return k


PARAM_NAMES = ['s5_a_re', 's5_a_im', 's5_log_step', 's5_b_re', 's5_b_im', 's5_c_re', 's5_c_im', 's5_d', 's5_w_out',
               's5_w_gate', 'pool_w', 'pool_scale', 'attn_w_qkv', 'attn_q_norm', 'attn_k_norm', 'attn_w_o',
               'ln1_g', 'ln1_b', 'ln2_g', 'ln2_b', 'mlp_w1', 'mlp_w2']


def kernel(x_prompt, x_sample, **params):
    x_prompt = np.asarray(x_prompt, np.float32)
    x_sample = np.asarray(x_sample, np.float32)
    nc = Builder(nseq=2).build()
    consts = make_consts()
    base = {n: np.ascontiguousarray(np.asarray(params[n], np.float32)) for n in PARAM_NAMES}
    base.update(consts)
    in_maps = []
    for i in range(8):
        m = dict(base)
        m['x_in'] = np.ascontiguousarray(np.stack([x_sample[i], x_prompt[i % 2]], axis=0))
        in_maps.append(m)
    res = run_bass_kernel_spmd(nc, in_maps, core_ids=list(range(8)))
    y_sample = np.stack([np.asarray(res.results[i]['y_out'][0]) for i in range(8)], axis=0).astype(np.float32)
    y_prompt = np.stack([np.asarray(res.results[i]['y_out'][1]) for i in range(2)], axis=0).astype(np.float32)
    return (y_prompt, y_sample)
```
